# Optimizing a Trainium2 kernel written in Bass

```python
import math
import jax
import jax.numpy as jnp
from jax import lax
import numpy as np

D_MODEL = 1024
BATCH = 32
SEQ = 2048
DEPTH = 2
DEC_BATCH = 16
DEC_SEQ = 64
PAST_LEN = 1024

CHUNK = 64
N_MIXERS = 2
N_ATTN_LAYERS = (DEPTH + 1) // 2
N_SSM_LAYERS = DEPTH // 2
N_HEADS = 16
N_KV_HEADS = 4
HEAD_DIM = D_MODEL // N_HEADS
Q_PER_KV = N_HEADS // N_KV_HEADS
QKV_DIM = (N_HEADS + 2 * N_KV_HEADS) * HEAD_DIM
ROT_DIM = HEAD_DIM // 4
ROPE_THETA = 500000.0
WINDOW = 128
BAND_CHUNKS = WINDOW // CHUNK
GROUP_SIZE = 16
N_GROUPS = D_MODEL // GROUP_SIZE
STATE_DIM = 64
D_FF = 2816
CONV_WIDTH = 3
NORM_EPS = 1e-6
NEG_INF = -1e30

kernel_name = 'swa_sink_s5_convffn_stream_step'


def rms_norm(x, g):
    x32 = x.astype(jnp.float32)
    y = x32 * lax.rsqrt(jnp.mean(x32 * x32, axis=-1, keepdims=True) + NORM_EPS) * g.astype(jnp.float32)
    return y.astype(x.dtype)


def partial_rope(x, pos):
    half = ROT_DIM // 2
    inv_freq = jnp.power(jnp.float32(ROPE_THETA), -jnp.arange(half, dtype=jnp.float32) * (2.0 / ROT_DIM))
    ang = pos.astype(jnp.float32)[:, None] * inv_freq[None, :]
    cos = jnp.cos(ang)[None, :, None, :]
    sin = jnp.sin(ang)[None, :, None, :]
    xf = x.astype(jnp.float32)
    x1 = xf[..., :half]
    x2 = xf[..., half:ROT_DIM]
    out = jnp.concatenate([x1 * cos - x2 * sin, x2 * cos + x1 * sin, xf[..., ROT_DIM:]], axis=-1)
    return out.astype(x.dtype)


def qkv_project(h, w_qkv, b_qkv, pos):
    b, t, _ = h.shape
    qkv = h @ w_qkv + b_qkv
    nq = N_HEADS * HEAD_DIM
    nk = N_KV_HEADS * HEAD_DIM
    q = qkv[..., :nq].reshape(b, t, N_HEADS, HEAD_DIM)
    k = qkv[..., nq:nq + nk].reshape(b, t, N_KV_HEADS, HEAD_DIM)
    v = qkv[..., nq + nk:].reshape(b, t, N_KV_HEADS, HEAD_DIM)
    return partial_rope(q, pos), partial_rope(k, pos), v


def sink_softmax(s, sink, mask=None):
    if mask is not None:
        s = jnp.where(mask, s, NEG_INF)
    m = jnp.maximum(jnp.max(s, axis=-1, keepdims=True), sink)
    p = jnp.exp(s - m)
    return p / (jnp.sum(p, axis=-1, keepdims=True) + jnp.exp(sink - m))


def attn_prompt(h, w_qkv, b_qkv, sinks, w_o, b_o):
    b, s_len, _ = h.shape
    n_c = s_len // CHUNK
    q, k, v = qkv_project(h, w_qkv, b_qkv, jnp.arange(s_len))
    qb = q.reshape(b, n_c, CHUNK, N_KV_HEADS, Q_PER_KV, HEAD_DIM)
    pad = ((0, 0), (BAND_CHUNKS * CHUNK, 0), (0, 0), (0, 0))
    kp = jnp.pad(k, pad).reshape(b, n_c + BAND_CHUNKS, CHUNK, N_KV_HEADS, HEAD_DIM)
    vp = jnp.pad(v, pad).reshape(b, n_c + BAND_CHUNKS, CHUNK, N_KV_HEADS, HEAD_DIM)
    kb = jnp.concatenate([kp[:, j:j + n_c] for j in range(BAND_CHUNKS + 1)], axis=2)
    vb = jnp.concatenate([vp[:, j:j + n_c] for j in range(BAND_CHUNKS + 1)], axis=2)
    sc = jnp.einsum('bnqhgd,bnkhd->bnhgqk', qb, kb).astype(jnp.float32) * (HEAD_DIM ** -0.5)
    key_pos = (jnp.arange(n_c)[:, None] - BAND_CHUNKS) * CHUNK + jnp.arange((BAND_CHUNKS + 1) * CHUNK)[None, :]
    mask = (key_pos >= 0)[None, :, None, None, None, :]
    sink = sinks.astype(jnp.float32).reshape(N_KV_HEADS, Q_PER_KV)[None, None, :, :, None, None]
    p = sink_softmax(sc, sink, mask)
    o = jnp.einsum('bnhgqk,bnkhd->bnqhgd', p.astype(vb.dtype), vb).reshape(b, s_len, D_MODEL)
    return o @ w_o + b_o, k[:, -WINDOW:], v[:, -WINDOW:]


def attn_sample(h, ck, cv, w_qkv, b_qkv, sinks, w_o, b_o):
    b, t, _ = h.shape
    q, k, v = qkv_project(h, w_qkv, b_qkv, PAST_LEN + jnp.arange(t))
    kk = jnp.concatenate([ck.astype(k.dtype), k], axis=1)
    vv = jnp.concatenate([cv.astype(v.dtype), v], axis=1)
    qg = q.reshape(b, t, N_KV_HEADS, Q_PER_KV, HEAD_DIM)
    sc = jnp.einsum('bqhgd,bkhd->bhgqk', qg, kk).astype(jnp.float32) * (HEAD_DIM ** -0.5)
    sink = sinks.astype(jnp.float32).reshape(N_KV_HEADS, Q_PER_KV)[None, :, :, None, None]
    p = sink_softmax(sc, sink)
    o = jnp.einsum('bhgqk,bkhd->bqhgd', p.astype(vv.dtype), vv).reshape(b, t, D_MODEL)
    w_rows = ck.shape[1]
    return o @ w_o + b_o, kk[:, -w_rows:], vv[:, -w_rows:]


def _affine_combine(e1, e2):
    ar1, ai1, br1, bi1 = e1
    ar2, ai2, br2, bi2 = e2
    return (ar2 * ar1 - ai2 * ai1,
            ar2 * ai1 + ai2 * ar1,
            ar2 * br1 - ai2 * bi1 + br2,
            ar2 * bi1 + ai2 * br1 + bi2)


def s5_discretize(lam_re, lam_im, log_dt, b_re, b_im):
    lr = lam_re.astype(jnp.float32)
    li = lam_im.astype(jnp.float32)
    dt = jnp.exp(log_dt.astype(jnp.float32))[:, None]
    mag = jnp.exp(lr * dt)
    lbar_re = mag * jnp.cos(li * dt)
    lbar_im = mag * jnp.sin(li * dt)
    nr = lbar_re - 1.0
    ni = lbar_im
    den = lr * lr + li * li
    cr = (nr * lr + ni * li) / den
    ci = (ni * lr - nr * li) / den
    br = b_re.astype(jnp.float32)
    bi = b_im.astype(jnp.float32)
    bbar_re = cr[..., None] * br - ci[..., None] * bi
    bbar_im = cr[..., None] * bi + ci[..., None] * br
    return lbar_re, lbar_im, bbar_re, bbar_im


def s5_mixer(h, h0_re, h0_im, lam_re, lam_im, log_dt, b_re, b_im, c_re, c_im, d_skip, w_glu, b_glu):
    b, t, _ = h.shape
    blk = min(t, CHUNK)
    n_blk = t // blk
    u = h.astype(jnp.float32).reshape(b, n_blk, blk, N_GROUPS, GROUP_SIZE).transpose(1, 0, 2, 3, 4)
    lr, li, bbr, bbi = s5_discretize(lam_re, lam_im, log_dt, b_re, b_im)
    cr = c_re.astype(jnp.float32)
    ci = c_im.astype(jnp.float32)

    def block(carry, u_blk):
        hr, hi = carry
        xr = jnp.einsum('blgi,gpi->blgp', u_blk, bbr)
        xi = jnp.einsum('blgi,gpi->blgp', u_blk, bbi)
        xr = xr.at[:, 0].add(lr * hr - li * hi)
        xi = xi.at[:, 0].add(lr * hi + li * hr)
        ar = jnp.broadcast_to(lr, xr.shape)
        ai = jnp.broadcast_to(li, xi.shape)
        _, _, sr, si = lax.associative_scan(_affine_combine, (ar, ai, xr, xi), axis=1)
        y = jnp.einsum('blgp,gip->blgi', sr, cr) - jnp.einsum('blgp,gip->blgi', si, ci)
        return (sr[:, -1], si[:, -1]), y

    (hr, hi), ys = lax.scan(block, (h0_re.astype(jnp.float32), h0_im.astype(jnp.float32)), u)
    y = ys.transpose(1, 0, 2, 3, 4).reshape(b, t, D_MODEL) + d_skip.astype(jnp.float32) * h.astype(jnp.float32)
    z = jax.nn.gelu(y).astype(h.dtype)
    ag = z @ w_glu + b_glu
    out = ag[..., :D_MODEL] * jax.nn.sigmoid(ag[..., D_MODEL:])
    return out, hr, hi


def conv_ffn(h, prev, w_up, conv_w, conv_b, w_down):
    t = h.shape[1]
    up = h @ w_up
    full = jnp.concatenate([prev.astype(up.dtype), up], axis=1)
    c = conv_b
    for j in range(CONV_WIDTH):
        c = c + conv_w[j] * full[:, j:j + t]
    y = (jax.nn.gelu(c[..., :D_FF]) * c[..., D_FF:]) @ w_down
    return y, full[:, -(CONV_WIDTH - 1):]


def setup_inputs(seed: int = 0) -> dict:
    key = jax.random.key(seed)
    ks = iter(jax.random.split(key, 40))
    f32 = jnp.float32

    def nrm(shape, scale=1.0):
        return jax.random.normal(next(ks), shape, f32) * scale

    na, ns = N_ATTN_LAYERS, N_SSM_LAYERS
    win_rows = min(WINDOW, PAST_LEN)
    lam_im = jnp.pi * jnp.arange(STATE_DIM, dtype=f32)[None, None, :] + nrm((ns, N_GROUPS, STATE_DIM), 0.01)
    return {
        'x_prompt': nrm((BATCH, SEQ, D_MODEL)),
        'x_sample': nrm((DEC_BATCH, DEC_SEQ, D_MODEL)),
        'cache_k': nrm((na, DEC_BATCH, win_rows, N_KV_HEADS, HEAD_DIM)),
        'cache_v': nrm((na, DEC_BATCH, win_rows, N_KV_HEADS, HEAD_DIM)),
        'state_ssm_re': nrm((ns, DEC_BATCH, N_GROUPS, STATE_DIM), 0.5),
        'state_ssm_im': nrm((ns, DEC_BATCH, N_GROUPS, STATE_DIM), 0.5),
        'cache_conv': nrm((DEPTH, DEC_BATCH, CONV_WIDTH - 1, 2 * D_FF)),
        'g_pre_mix': 1.0 + nrm((DEPTH, D_MODEL), 0.05),
        'g_post_mix': 1.0 + nrm((DEPTH, D_MODEL), 0.05),
        'g_pre_ffn': 1.0 + nrm((DEPTH, D_MODEL), 0.05),
        'g_post_ffn': 1.0 + nrm((DEPTH, D_MODEL), 0.05),
        'w_qkv': nrm((na, D_MODEL, QKV_DIM), D_MODEL ** -0.5),
        'b_qkv': nrm((na, QKV_DIM), 0.02),
        'attn_sinks': nrm((na, N_HEADS), 0.5),
        'w_o': nrm((na, D_MODEL, D_MODEL), D_MODEL ** -0.5),
        'b_o': nrm((na, D_MODEL), 0.02),
        'ssm_lam_re': -0.5 + nrm((ns, N_GROUPS, STATE_DIM), 0.01),
        'ssm_lam_im': lam_im,
        'ssm_log_dt': jax.random.uniform(next(ks), (ns, N_GROUPS), f32, math.log(1e-3), math.log(1e-1)),
        'ssm_b_re': nrm((ns, N_GROUPS, STATE_DIM, GROUP_SIZE), (2 * GROUP_SIZE) ** -0.5),
        'ssm_b_im': nrm((ns, N_GROUPS, STATE_DIM, GROUP_SIZE), (2 * GROUP_SIZE) ** -0.5),
        'ssm_c_re': nrm((ns, N_GROUPS, GROUP_SIZE, STATE_DIM), (2 * STATE_DIM) ** -0.5),
        'ssm_c_im': nrm((ns, N_GROUPS, GROUP_SIZE, STATE_DIM), (2 * STATE_DIM) ** -0.5),
        'ssm_d': nrm((ns, D_MODEL)),
        'w_glu': nrm((ns, D_MODEL, 2 * D_MODEL), D_MODEL ** -0.5),
        'b_glu': nrm((ns, 2 * D_MODEL), 0.02),
        'w_up': nrm((DEPTH, D_MODEL, 2 * D_FF), D_MODEL ** -0.5),
        'conv_w': nrm((DEPTH, CONV_WIDTH, 2 * D_FF), CONV_WIDTH ** -0.5),
        'conv_b': nrm((DEPTH, 2 * D_FF), 0.02),
        'w_down': nrm((DEPTH, D_FF, D_MODEL), D_FF ** -0.5),
    }


def reference(x_prompt, x_sample, cache_k, cache_v, state_ssm_re, state_ssm_im, cache_conv,
              g_pre_mix, g_post_mix, g_pre_ffn, g_post_ffn,
              w_qkv, b_qkv, attn_sinks, w_o, b_o,
              ssm_lam_re, ssm_lam_im, ssm_log_dt, ssm_b_re, ssm_b_im, ssm_c_re, ssm_c_im, ssm_d, w_glu, b_glu,
              w_up, conv_w, conv_b, w_down):
    xp, xs = x_prompt, x_sample
    bp = xp.shape[0]
    k_p, v_p, k_s, v_s = [], [], [], []
    re_p, im_p, re_s, im_s = [], [], [], []
    conv_p, conv_s = [], []
    for i in range(DEPTH):
        j = i // N_MIXERS
        hp = rms_norm(xp, g_pre_mix[i])
        hs = rms_norm(xs, g_pre_mix[i])
        if i % N_MIXERS == 0:
            mp, kpi, vpi = attn_prompt(hp, w_qkv[j], b_qkv[j], attn_sinks[j], w_o[j], b_o[j])
            ms, ksi, vsi = attn_sample(hs, cache_k[j], cache_v[j], w_qkv[j], b_qkv[j], attn_sinks[j], w_o[j], b_o[j])
            k_p.append(kpi)
            v_p.append(vpi)
            k_s.append(ksi)
            v_s.append(vsi)
        else:
            ssm_params = (ssm_lam_re[j], ssm_lam_im[j], ssm_log_dt[j], ssm_b_re[j], ssm_b_im[j],
                          ssm_c_re[j], ssm_c_im[j], ssm_d[j], w_glu[j], b_glu[j])
            h0 = jnp.zeros((bp, N_GROUPS, STATE_DIM), jnp.float32)
            mp, hrp, hip = s5_mixer(hp, h0, h0, *ssm_params)
            ms, hrs, his = s5_mixer(hs, state_ssm_re[j], state_ssm_im[j], *ssm_params)
            re_p.append(hrp)
            im_p.append(hip)
            re_s.append(hrs)
            im_s.append(his)
        xp = xp + rms_norm(mp, g_post_mix[i])
        xs = xs + rms_norm(ms, g_post_mix[i])
        hp = rms_norm(xp, g_pre_ffn[i])
        hs = rms_norm(xs, g_pre_ffn[i])
        fp, cpi = conv_ffn(hp, jnp.zeros((bp, CONV_WIDTH - 1, 2 * D_FF), hp.dtype), w_up[i], conv_w[i], conv_b[i], w_down[i])
        fs, csi = conv_ffn(hs, cache_conv[i], w_up[i], conv_w[i], conv_b[i], w_down[i])
        conv_p.append(cpi)
        conv_s.append(csi)
        xp = xp + rms_norm(fp, g_post_ffn[i])
        xs = xs + rms_norm(fs, g_post_ffn[i])
    return (xp, xs,
            jnp.stack(k_p), jnp.stack(v_p), jnp.stack(k_s), jnp.stack(v_s),
            jnp.stack(re_p), jnp.stack(im_p), jnp.stack(re_s), jnp.stack(im_s),
            jnp.stack(conv_p), jnp.stack(conv_s))
```

```python
import numpy as np
import concourse.bass as bass
import concourse.mybir as mybir
from concourse.bass_utils import run_bass_kernel_spmd

F32 = mybir.dt.float32
BF16 = mybir.dt.bfloat16
I32 = mybir.dt.int32
U8 = mybir.dt.uint8
AF = mybir.ActivationFunctionType
ALU = mybir.AluOpType

D = 1024
KC = 8
DFF = 2816
NPAIR = 22
NUP = 44
NG = 64
PAST = 1024
EPS = 1e-6
SCALE = 0.125
N_CORES = 8

V_G = 0
V_BQK = 64
V_BO = 88
V_BGL = 96
V_SD = 112
V_CB = 120
V_CW = 208
V_EPS = 472
V_N = 473


class Buf:
    __slots__ = ("w", "r", "excl")

    def __init__(self, excl=False):
        self.w = {}
        self.r = {}
        self.excl = excl

    def inherit(self, others):
        for o in others:
            for s, v in o.w.items():
                if self.w.get(s, 0) < v:
                    self.w[s] = v
            for s, v in o.r.items():
                if self.r.get(s, 0) < v:
                    self.r[s] = v
        return self


def bufs(*shape):
    if len(shape) == 1:
        return [Buf() for _ in range(shape[0])]
    return [bufs(*shape[1:]) for _ in range(shape[0])]


class Prog:
    ENGS = ("pe", "act", "dve", "pool", "sp")

    def __init__(self, nc):
        self.nc = nc
        self.eobj = dict(pe=nc.tensor, act=nc.scalar, dve=nc.vector, pool=nc.gpsimd, sp=nc.sync)
        self.sems = []
        self.stream = {e: [] for e in self.ENGS}
        self.esem = {e: self._newsem("e_" + e) for e in self.ENGS}
        self.cnt = {e: 0 for e in self.ENGS}
        self.waited = {e: {} for e in self.ENGS}
        self.NDS = 16
        self.dsem = {q: [self._newsem("d_%s%d" % (q, i)) for i in range(self.NDS)] for q in ("sp", "pool", "act")}
        self.dcnt = {q: 0 for q in self.dsem}
        self.nbank = 0

    def _newsem(self, name):
        self.sems.append(self.nc.alloc_semaphore(name))
        return len(self.sems) - 1

    def _collect(self, eng, reads, writes, extra=None):
        deps = dict(extra) if extra else {}
        own = self.esem.get(eng, -1)
        for b in reads:
            for s, v in b.w.items():
                if deps.get(s, 0) < v:
                    deps[s] = v
            if b.excl:
                for s, v in b.r.items():
                    if s != own and deps.get(s, 0) < v:
                        deps[s] = v
        for b in writes:
            for s, v in b.w.items():
                if deps.get(s, 0) < v:
                    deps[s] = v
            for s, v in b.r.items():
                if deps.get(s, 0) < v:
                    deps[s] = v
        wd = self.waited[eng]
        waits = []
        pe_self = self.esem["pe"] if eng == "pe" else -1
        for s, v in deps.items():
            if s == pe_self:
                continue
            if wd.get(s, 0) < v:
                wd[s] = v
                waits.append((s, v))
        return waits

    def _mark(self, tok, reads, writes):
        s, v = tok
        for b in reads:
            b.r[s] = v
        for b in writes:
            b.w = {s: v}
            b.r = {}

    @staticmethod
    def _excl(reads, writes):
        if any(b.excl for b in reads):
            writes = list(writes) + [b for b in reads if b.excl]
            reads = [b for b in reads if not b.excl]
        return reads, writes

    def op(self, eng, fn, reads=(), writes=()):
        waits = self._collect(eng, reads, writes)
        self.cnt[eng] += 1
        tok = (self.esem[eng], self.cnt[eng])
        self.stream[eng].append((waits, fn, (tok[0], 1)))
        self._mark(tok, reads, writes)
        return tok

    def dma(self, q, fn, reads=(), writes=()):
        k = self.dcnt[q]
        self.dcnt[q] += 1
        slot = k % self.NDS
        use = k // self.NDS
        s = self.dsem[q][slot]
        extra = {s: 16 * use} if use > 0 else None
        waits = self._collect(q, reads, writes, extra)
        tok = (s, 16 * (use + 1))
        self.stream[q].append((waits, fn, (s, 16)))
        self._mark(tok, reads, writes)
        return tok

    def all_tokens(self):
        tgt = {}
        for e in self.ENGS:
            if self.cnt[e] > 0:
                tgt[self.esem[e]] = self.cnt[e]
        for q in self.dsem:
            k = self.dcnt[q]
            for slot in range(self.NDS):
                uses = (k - slot + self.NDS - 1) // self.NDS
                if uses > 0:
                    tgt[self.dsem[q][slot]] = 16 * uses
        return tgt

    def fence(self, engs=None):
        tgt = self.all_tokens()
        for e in (engs or self.ENGS):
            wd = self.waited[e]
            waits = []
            for s, v in tgt.items():
                if e == "pe" and s == self.esem["pe"]:
                    continue
                if wd.get(s, 0) < v:
                    wd[s] = v
                    waits.append((s, v))
            if waits:
                self.stream[e].append((waits, None, None))

    def emit(self):
        nc = self.nc
        with nc.Block() as block:
            def mk(e):
                def body(_):
                    eo = self.eobj[e]
                    for waits, fn, inc in self.stream[e]:
                        if fn is None:
                            for s, v in waits:
                                eo.wait_ge(self.sems[s], v)
                            continue
                        for s, v in waits[:-1]:
                            eo.wait_ge(self.sems[s], v)
                        ins = fn(eo)
                        if waits:
                            s, v = waits[-1]
                            ins.wait_op(self.sems[s], v, "sem-ge")
                        ins.then_inc(self.sems[inc[0]], inc[1])
                return body
            block.tensor(mk("pe"))
            block.scalar(mk("act"))
            block.vector(mk("dve"))
            block.gpsimd(mk("pool"))
            block.sync(mk("sp"))


def MM(P, out, lhsT, rhs, start, stop, reads, writes):
    return P.op("pe", lambda e: e.matmul(out, lhsT=lhsT, rhs=rhs, start=start, stop=stop), reads, writes)


def TR(P, out, in_, ident, reads, writes):
    return P.op("pe", lambda e: e.transpose(out, in_, ident), reads, writes)


def ACT(P, out, in_, func, reads, writes, bias=None, scale=None):
    kw = {}
    if bias is not None:
        kw["bias"] = bias
    if scale is not None:
        kw["scale"] = scale
    return P.op("act", lambda e: e.activation(out=out, in_=in_, func=func, **kw), reads, writes)


def TT(P, eng, out, in0, in1, op, reads, writes):
    return P.op(eng, lambda e: e.tensor_tensor(out=out, in0=in0, in1=in1, op=op), reads, writes)


def STT(P, out, in0, scalar, in1, op0, op1, reads, writes):
    return P.op("dve", lambda e: e.scalar_tensor_tensor(out=out, in0=in0, scalar=scalar, in1=in1, op0=op0, op1=op1),
                reads, writes)


def TS(P, eng, out, in0, s1, op0, reads, writes, s2=None, op1=None):
    if op1 is None:
        return P.op(eng, lambda e: e.tensor_scalar(out=out, in0=in0, scalar1=s1, scalar2=None, op0=op0), reads, writes)
    return P.op(eng, lambda e: e.tensor_scalar(out=out, in0=in0, scalar1=s1, scalar2=s2, op0=op0, op1=op1), reads, writes)


def CP(P, eng, out, in_, reads, writes):
    if eng == "act":
        return P.op("act", lambda e: e.activation(out=out, in_=in_, func=AF.Copy), reads, writes)
    return P.op(eng, lambda e: e.tensor_copy(out=out, in_=in_), reads, writes)


def DMA(P, q, out, in_, reads, writes, **kw):
    return P.dma(q, lambda e: e.dma_start(out=out, in_=in_, **kw), reads, writes)


class K:
    pass


def build(NP, SEQ, NS, DEC, nlayers=2, dbg=None, wplan=None):
    nc = bass.Bass("TRN2", target_bir_lowering=False)
    P = Prog(nc)
    k = K()
    k.nc, k.P, k.NP, k.SEQ, k.NS, k.DEC = nc, P, NP, SEQ, NS, DEC
    TTP = min(1024, SEQ)
    TMAX = max(TTP, DEC)
    k.TMAX = TMAX

    def din(name, shape, dt=F32):
        return nc.dram_tensor(name, list(shape), dt, kind="ExternalInput").ap()

    def dout(name, shape, dt=F32):
        return nc.dram_tensor(name, list(shape), dt, kind="ExternalOutput").ap()

    def dscr(name, shape, dt=BF16):
        return nc.dram_tensor(name, list(shape), dt, kind="Internal").ap()

    k.xp = din("xp", [NP, SEQ, D])
    k.xs = din("xs", [NS, DEC, D])
    k.ck = din("ck", [NS, 128, 256])
    k.cv = din("cv", [NS, 128, 256])
    k.sre = din("sre", [NS, 64, 64])
    k.sim = din("sim", [NS, 64, 64])
    k.cconv = din("cconv", [2, NS, 2, 2 * DFF])
    k.wA = din("wA", [24, 128, 1024])
    k.wV = din("wV", [128, 2048])
    k.wO = din("wO", [8, 128, 1024])
    k.wUP = din("wUP", [2, NUP, 128, 1024])
    k.wDN = din("wDN", [2, 8, 128, DFF])
    k.wGL = din("wGL", [16, 128, 1024])
    k.vecs_d = din("vecs", [128, V_N])
    k.bv_d = din("bvbc", [128, 256])
    k.sink_d = din("sinkarr", [128, 8])
    k.ident_d = din("ident", [128, 128])
    k.rope_d = din("rope", [2, 128, SEQ + DEC])
    k.mask_d = din("ssmmask", [128, 128])
    k.perm_d = din("permsw", [128, 128])
    k.lam_d = din("ssmlam", [3, 64, 64])
    k.bc_d = din("ssmbc", [4, 64, 64 * 16])

    k.yp = dout("yp", [NP, SEQ, D])
    k.ys = dout("ys", [NS, DEC, D])
    k.kp = dout("kp", [NP, 128, 256])
    k.vp = dout("vp", [NP, 128, 256])
    k.ks = dout("ks", [NS, 128, 256])
    k.vs = dout("vs", [NS, 128, 256])
    k.rep = dout("rep", [NP, 64, 64])
    k.imp = dout("imp", [NP, 64, 64])
    k.res = dout("res", [NS, 64, 64])
    k.ims = dout("ims", [NS, 64, 64])
    k.cvp = dout("cvp", [2, NP, 2, 2 * DFF])
    k.cvs = dout("cvs", [2, NS, 2, 2 * DFF])

    k.wAb = dscr("wAb", [24, 128, 1024])
    k.wVb = dscr("wVb", [128, 2048])
    k.wOb = dscr("wOb", [8, 128, 1024])
    k.wUPb = dscr("wUPb", [2, NUP, 128, 1024])
    k.wDNb = dscr("wDNb", [2, 8, 128, DFF])
    k.wGLb = dscr("wGLb", [16, 128, 1024])
    k.ssmW = dscr("ssmW", [NG, 128, 256])
    k.ssmG = dscr("ssmG", [NG // 2, 128, 256])
    k.a8d = dscr("a8d", [5, 64, 64], F32)
    k.rtab = dscr("rtab", [2, 128, 32 * 65], F32)
    k.b_rtab = Buf()
    k.b_wA = bufs(24)
    k.b_wV = Buf()
    k.b_wO = bufs(8)
    k.b_wUP = bufs(2, NUP)
    k.b_wDN = bufs(2, 8)
    k.b_wGL = bufs(16)
    k.b_ssmW = bufs(NG)
    k.b_ssmG = bufs(NG // 2)
    k.b_a8d = Buf()

    def sb(name, shape, dt):
        return nc.alloc_sbuf_tensor(name, list(shape), dt)

    k.xT = sb("xT", [128, KC, TMAX], F32)
    k.b_xT = bufs(KC, 2)
    k.hT = sb("hT", [128, KC, TMAX], BF16)
    k.b_hT = bufs(KC, 2)
    k.mT = sb("mT", [128, KC, TMAX], BF16)
    k.b_mT = bufs(KC, 2)
    k.rstd = sb("rstd", [128, TMAX], F32)
    k.b_rstd = bufs(2)
    k.sq = sb("sq", [128, 4, 512], BF16)
    k.b_sq = bufs(4)
    k.sqi = 0
    k.ring1 = sb("ring1", [128, 8, 1024], BF16)
    k.b_ring1 = bufs(8)
    k.r1i = 0
    k.ring2 = sb("ring2", [128, 3, DFF], BF16)
    k.b_ring2 = bufs(3)
    k.r2i = 0
    k.vecs = sb("vecs_s", [128, V_N], F32)
    k.b_vecs = Buf()
    k.identf = sb("identf", [128, 128], F32)
    k.identb = sb("identb", [128, 128], BF16)
    k.onesb = sb("onesb", [128, 128], BF16)
    k.permb = sb("permb", [128, 128], BF16)
    k.b_const = Buf()
    k.xsi = 0
    k.tails = sb("tails", [128, 2, NUP, 3, 2], F32)
    k.b_tails = bufs(2, NUP, 3)
    k.khist = sb("khist", [128, 4, 128], BF16)
    k.b_khist = Buf()
    k.vhist = sb("vhist", [128, 2, 256], BF16)
    k.b_vhist = Buf()
    k.expsink = sb("expsink", [128, 8, 64], F32)
    k.bvbc = sb("bvbc_s", [128, 256], F32)
    k.hstate = sb("hstate", [128, 3, 32], F32)
    k.b_hstate = Buf()
    k.a12 = sb("a12", [128, 2, 2, 32], F32)
    k.rot1 = sb("rot1", [128, 3, 32], F32)
    k.ARENA = 83 * 1024
    k.arena = sb("arena", [128, k.ARENA], U8)
    k.ps = [nc.alloc_psum_tensor("ps%d" % i, [128, 512], F32) for i in range(8)]
    k.b_ps = [Buf(excl=True) for _ in range(8)]
    k.dbg = dbg or {}
    k.arena_sum = Buf()
    k.cur_arena = []
    k.pend = []
    k.tpar = [0, 0]
    k.record = wplan is None
    k.wplan = wplan if wplan is not None else {1: [], 2: []}
    k.nlayers = nlayers
    k.tmpA = sb("tmpA", [128, 2, 512], F32)
    k.b_tmpA = bufs(2)
    k.tmpi = 0
    k.dbg_out = {}
    return k


def carve(k, off, shape, dt):
    esz = 4 if dt in (F32, I32) else 2
    n = int(np.prod(shape))
    nb = n * esz
    off = (off + 31) // 32 * 32
    assert off + nb <= k.ARENA, (off, nb, k.ARENA)
    ap = k.arena[:, off:off + nb].bitcast(dt)
    if len(shape) == 2:
        ap = ap.rearrange("p (a b) -> p a b", b=shape[1])
    elif len(shape) == 3:
        ap = ap.rearrange("p (a b c) -> p a b c", b=shape[1], c=shape[2])
    elif len(shape) == 4:
        ap = ap.rearrange("p (a b c d) -> p a b c d", b=shape[1], c=shape[2], d=shape[3])
    return ap, off + nb


def arena_switch(k):
    k.arena_sum.inherit(k.cur_arena)
    k.cur_arena = []
    import os
    if os.environ.get("KFENCE"):
        k.P.fence()


def nb(k, *shape):
    def mk():
        b = Buf().inherit([k.arena_sum])
        k.cur_arena.append(b)
        return b
    if len(shape) == 0:
        return mk()
    if len(shape) == 1:
        return [mk() for _ in range(shape[0])]
    return [nb(k, *shape[1:]) for _ in range(shape[0])]


def bank(k, n=6):
    i = k.P.nbank % n
    k.P.nbank += 1
    return k.ps[i], k.b_ps[i]


def RECIP(P, out, in_, reads, writes):
    return P.op("dve", lambda e: e.reciprocal(out=out, in_=in_), reads, writes)


def MEMSET(P, eng, ap, val, writes):
    return P.op(eng, lambda e: e.memset(ap, val), [], writes)


class Ring:
    def __init__(self, tensor, bufs_, nslot):
        self.tensor, self.bufs, self.nslot = tensor, bufs_, nslot
        self.issued = 0
        self.used = 0


def wsrc(k, key):
    n = key[0]
    if n == "A":
        return k.wAb[key[1]], [k.b_wA[key[1]]]
    if n == "V":
        return k.wVb[:, :], [k.b_wV]
    if n == "O":
        return k.wOb[key[1]], [k.b_wO[key[1]]]
    if n == "U":
        return k.wUPb[key[1], key[2]], [k.b_wUP[key[1]][key[2]]]
    if n == "D":
        return k.wDNb[key[1], key[2]], [k.b_wDN[key[1]][key[2]]]
    if n == "G":
        return k.wGLb[key[1]], [k.b_wGL[key[1]]]
    if n == "SW":
        g0 = key[1]
        return k.ssmW[g0:g0 + 8].rearrange("g p c -> p g c"), k.b_ssmW[g0:g0 + 8]
    if n == "SG":
        g0 = key[1]
        return k.ssmG[g0:g0 + 4].rearrange("g p c -> p g c"), k.b_ssmG[g0:g0 + 4]
    raise KeyError(key)


def _issue_w(k, ring, upto):
    R = k.rings[ring]
    lst = k.wplan[ring]
    upto = min(upto, len(lst) - 1)
    while R.issued <= upto:
        j = R.issued
        src, sb_ = wsrc(k, lst[j])
        s = j % R.nslot
        shp = src.shape
        fs = int(np.prod(shp[1:]))
        dst = R.tensor[:, s, 0:fs]
        if len(shp) == 3:
            dst = dst.rearrange("p (a b) -> p a b", b=shp[2])
        DMA(k.P, "sp", dst, src, sb_, [R.bufs[s]])
        R.issued += 1


def getw(k, ring, key):
    R = k.rings[ring]
    i = R.used
    R.used += 1
    if k.record:
        k.wplan[ring].append(key)
    else:
        assert k.wplan[ring][i] == key, (k.wplan[ring][i], key)
        _issue_w(k, ring, i + R.nslot - 2)
    s = i % R.nslot
    src, _ = wsrc(k, key)
    fs = int(np.prod(src.shape[1:]))
    return R.tensor[:, s, 0:fs], R.bufs[s]


class Tile:
    def __init__(self, kind, idx, t0, TT, first, last, pos0):
        self.kind, self.idx, self.t0, self.TT, self.first, self.last, self.pos0 = kind, idx, t0, TT, first, last, pos0
        self.NH = max(1, TT // 512)
        self.HC = min(TT, 512)
        self.BT = min(128, TT)
        self.NBLK = TT // self.BT


def setup(k):
    P = k.P
    nc = k.nc
    k.rings = {1: Ring(k.ring1, k.b_ring1, 8), 2: Ring(k.ring2, k.b_ring2, 3)}
    DMA(P, "act", k.vecs[:], k.vecs_d[:, :], [], [k.b_vecs])
    DMA(P, "act", k.identf[:], k.ident_d[:, :], [], [k.b_const])
    DMA(P, "act", k.bvbc[:], k.bv_d[:, :], [], [k.b_const])
    CP(P, "dve", k.identb[:], k.identf[:], [k.b_const], [k.b_const])
    MEMSET(P, "dve", k.onesb[:], 1.0, [k.b_const])
    pf, _o = carve(k, 2048, [128], F32)
    bpf = nb(k)
    DMA(P, "act", pf, k.perm_d[:, :], [], [bpf])
    CP(P, "dve", k.permb[:], pf, [bpf], [k.b_const])
    zt, o = carve(k, 0, [64], F32)
    sk, o = carve(k, o, [8], F32)
    bz = nb(k)
    MEMSET(P, "dve", zt, 0.0, [bz])
    DMA(P, "act", sk, k.sink_d[:, :], [], [bz])
    ACT(P, sk, sk, AF.Exp, [bz], [bz])
    for j in range(8):
        TS(P, "dve", k.expsink[:, j, :], zt, sk[:, j:j + 1], ALU.add, [bz], [k.b_const])
    def conv(dst, src, dbufs):
        DMA(P, "pool", dst, src, [], dbufs)
    for a in range(0, 24, 2):
        conv(k.wAb[a:a + 1], k.wA[a:a + 1], k.b_wA[a:a + 1])
    conv(k.wVb[:, :], k.wV[:, :], [k.b_wV])
    conv(k.wOb[:], k.wO[:], k.b_wO)
    for l in range(2):
        for a in range(0, NUP, 8):
            b = min(NUP, a + 8)
            conv(k.wUPb[l, a:b], k.wUP[l, a:b], k.b_wUP[l][a:b])
        for a in range(0, 8, 4):
            conv(k.wDNb[l, a:a + 4], k.wDN[l, a:a + 4], k.b_wDN[l][a:a + 4])
        if l == 0:
            for a in range(0, 16, 8):
                conv(k.wGLb[a:a + 8], k.wGL[a:a + 8], k.b_wGL[a:a + 8])


def load_x(k, T):
    P = k.P
    BT = T.BT
    arena_switch(k)
    k.xstage, _ = carve(k, 0, [2, 1024], F32)
    k.b_xstage = nb(k, 2)
    src = k.xp if T.kind == "p" else k.xs
    for j in range(T.NBLK):
        s = k.xsi % 2
        k.xsi += 1
        DMA(P, "pool", k.xstage[0:BT, s, :], src[T.idx, T.t0 + j * BT:T.t0 + (j + 1) * BT, :], [], [k.b_xstage[s]])
        h = (j * BT) // 512
        for hb in range(2):
            ps, pb = bank(k)
            for q in range(4):
                kc = hb * 4 + q
                TR(P, ps[:, q * BT:(q + 1) * BT], k.xstage[0:BT, s, kc * 128:(kc + 1) * 128], k.identf[0:BT, 0:BT],
                   [k.b_xstage[s], k.b_const], [pb])
            CP(P, "act" if hb == 0 else "dve", k.xT[:, hb * 4:hb * 4 + 4, j * BT:(j + 1) * BT],
               ps[:, 0:4 * BT].rearrange("p (q t) -> p q t", t=BT), [pb], [k.b_xT[hb * 4 + q][h] for q in range(4)])


def store_y(k, T):
    P = k.P
    BT = T.BT
    arena_switch(k)
    k.xstage, _ = carve(k, 0, [2, 1024], F32)
    k.b_xstage = nb(k, 2)
    dst = k.yp if T.kind == "p" else k.ys
    for j in range(T.NBLK):
        s = k.xsi % 2
        k.xsi += 1
        h = (j * BT) // 512
        for hb in range(2):
            ps, pb = bank(k)
            for q in range(4):
                kc = hb * 4 + q
                TR(P, ps[0:BT, q * 128:(q + 1) * 128], k.xT[:, kc, j * BT:(j + 1) * BT], k.identf[:, :],
                   [k.b_xT[kc][h], k.b_const], [pb])
            CP(P, "act" if hb == 0 else "dve", k.xstage[0:BT, s, hb * 512:(hb + 1) * 512], ps[0:BT, 0:512],
               [pb], [k.b_xstage[s]])
        DMA(P, "act", dst[T.idx, T.t0 + j * BT:T.t0 + (j + 1) * BT, :], k.xstage[0:BT, s, :], [k.b_xstage[s]], [])


def finish_rstd(k, T, h):
    P = k.P
    c0, c1 = h * T.HC, (h + 1) * T.HC
    ssb, ssbuf = k.ps[6 + h], k.b_ps[6 + h]
    ACT(P, k.rstd[:, c0:c1], ssb[:, 0:T.HC], AF.Sqrt, [ssbuf, k.b_vecs], [k.b_rstd[h]],
        bias=k.vecs[:, V_EPS:V_EPS + 1], scale=1.0 / D)
    RECIP(P, k.rstd[:, c0:c1], k.rstd[:, c0:c1], [k.b_rstd[h]], [k.b_rstd[h]])


def prenorm(k, T, vec):
    P = k.P
    HC = T.HC
    for h in range(T.NH):
        c0, c1 = h * HC, (h + 1) * HC
        for kc in range(KC):
            si = k.sqi % 4
            k.sqi += 1
            ACT(P, k.sq[:, si, 0:HC], k.xT[:, kc, c0:c1], AF.Square, [k.b_xT[kc][h]], [k.b_sq[si]])
            MM(P, k.ps[6 + h][:, 0:HC], k.onesb[:, :], k.sq[:, si, 0:HC], kc == 0, kc == KC - 1,
               [k.b_sq[si], k.b_const], [k.b_ps[6 + h]])
        finish_rstd(k, T, h)
        for kc in range(KC):
            g = k.vecs[:, V_G + vec * 8 + kc:V_G + vec * 8 + kc + 1]
            STT(P, k.hT[:, kc, c0:c1], k.xT[:, kc, c0:c1], g, k.rstd[:, c0:c1], ALU.mult, ALU.mult,
                [k.b_xT[kc][h], k.b_rstd[h], k.b_vecs], [k.b_hT[kc][h]])


def post_evac(k, T, m, h, src, src_bufs, bias):
    P = k.P
    HC = T.HC
    c0, c1 = h * HC, (h + 1) * HC
    rd = list(src_bufs) + [k.b_vecs]
    ACT(P, k.mT[:, m, c0:c1], src, AF.Identity, rd, [k.b_mT[m][h]], bias=bias)
    si = k.sqi % 4
    k.sqi += 1
    ACT(P, k.sq[:, si, 0:HC], src, AF.Square, rd, [k.b_sq[si]], bias=bias)
    k.pend.append((h, si, HC, m == 0, m == KC - 1))
    import os
    while len(k.pend) > int(os.environ.get("KPEND", "2")):
        flush_ss(k, 1)


def flush_ss(k, n=None):
    P = k.P
    while k.pend and (n is None or n > 0):
        h, si, HC, st, sp = k.pend.pop(0)
        MM(P, k.ps[6 + h][:, 0:HC], k.onesb[:, :], k.sq[:, si, 0:HC], st, sp, [k.b_sq[si], k.b_const], [k.b_ps[6 + h]])
        if n is not None:
            n -= 1


def post_finish(k, T, vec):
    P = k.P
    HC = T.HC
    flush_ss(k)
    for h in range(T.NH):
        c0, c1 = h * HC, (h + 1) * HC
        finish_rstd(k, T, h)
        for m in range(KC):
            g = k.vecs[:, V_G + vec * 8 + m:V_G + vec * 8 + m + 1]
            ti = k.tmpi % 2
            k.tmpi += 1
            STT(P, k.tmpA[:, ti, 0:HC], k.mT[:, m, c0:c1], g, k.rstd[:, c0:c1], ALU.mult, ALU.mult,
                [k.b_mT[m][h], k.b_rstd[h], k.b_vecs], [k.b_tmpA[ti]])
            TT(P, "pool", k.xT[:, m, c0:c1], k.xT[:, m, c0:c1], k.tmpA[:, ti, 0:HC], ALU.add,
               [k.b_tmpA[ti], k.b_xT[m][h]], [k.b_xT[m][h]])


def attn_stage(k, T):
    P = k.P
    TT_, NH, HC, BT, NBLK = T.TT, T.NH, T.HC, T.BT, T.NBLK
    NCK = TT_ // 64
    NCH = HC // 64
    NEW = min(128, TT_)
    import os
    sub = int(os.environ.get("KSUB", "9"))
    prenorm(k, T, 0)
    if sub < 2:
        return
    arena_switch(k)
    o = 0
    qT, o = carve(k, o, [4, NCK, 2, 64], BF16)
    kdup, o = carve(k, o, [4, 128 + TT_], BF16)
    Vb, o = carve(k, o, [NCK + 2, 256], BF16)
    oT, o = carve(k, o, [KC, TT_], BF16)
    cosT, o = carve(k, o, [TT_], F32)
    sinT, o = carve(k, o, [TT_], F32)
    t1, o = carve(k, o, [3, HC], F32)
    t2, o = carve(k, o, [3, HC], F32)
    abf, o = carve(k, o, [3, HC], BF16)
    pt, o = carve(k, o, [2, 2, 3, 128], BF16)
    kf, o = carve(k, o, [4, NEW], F32)
    vf, o = carve(k, o, [2, 256], F32)
    dn, o = carve(k, o, [2, 128], F32)
    kcb, o = carve(k, o, [256], BF16)
    kout, o = carve(k, o, [256], F32)
    b_q = nb(k, 8, NH)
    b_k = nb(k, 4, NH)
    b_kh = nb(k)
    b_V = nb(k, NCK + 2)
    b_oT = nb(k, KC, NH)
    b_rope = nb(k)
    b_t1 = nb(k, 3)
    b_t2 = nb(k, 3)
    b_abf = nb(k, 3)
    b_pt = nb(k, 2)
    b_kf = nb(k)
    b_vf = nb(k)
    b_dn = nb(k, 2)
    b_kcb = nb(k)
    b_kout = nb(k)
    DMA(P, "pool", cosT, k.rope_d[0, :, T.pos0:T.pos0 + TT_], [], [b_rope])
    DMA(P, "pool", sinT, k.rope_d[1, :, T.pos0:T.pos0 + TT_], [], [b_rope])
    hist = not (T.kind == "p" and T.first)
    if T.kind == "s":
        DMA(P, "pool", kcb, k.ck[T.idx], [], [b_kcb])
        DMA(P, "pool", Vb[0:64, 0, :], k.cv[T.idx, 0:64, :], [], [b_V[0]])
        DMA(P, "pool", Vb[0:64, 1, :], k.cv[T.idx, 64:128, :], [], [b_V[1]])
        ps, pb = bank(k)
        psb = ps[:, :].bitcast(BF16)
        for hk in range(4):
            for e in range(2):
                TR(P, psb[64 * e:64 * e + 64, hk * 128:(hk + 1) * 128], kcb[:, hk * 64:(hk + 1) * 64], k.identb[:, :],
                   [b_kcb, k.b_const], [pb])
        CP(P, "dve", kdup[:, :, 0:128], psb[:, 0:512].rearrange("p (h t) -> p h t", t=128), [pb], [b_kh])
        DMA(P, "act", k.ks[T.idx, 0:64, :], k.ck[T.idx, 64:128, :], [], [])
        DMA(P, "act", k.vs[T.idx, 0:64, :], k.cv[T.idx, 64:128, :], [], [])
    elif hist:
        CP(P, "pool", kdup[:, :, 0:128], k.khist[:, :, :], [k.b_khist], [b_kh])
        CP(P, "pool", Vb[0:64, 0:2, :], k.vhist[0:64, :, :], [k.b_vhist], [b_V[0], b_V[1]])

    if sub < 3:
        return
    def rope_finish(item):
        i, h, ai, ti = item
        c0, c1 = h * HC, (h + 1) * HC
        psw, pbw = bank(k)
        MM(P, psw[:, 0:HC], k.permb[:, :], abf[:, ai, :], True, True, [b_abf[ai], k.b_const], [pbw])
        TT(P, "dve", t2[:, ti, :], psw[:, 0:HC], sinT[:, c0:c1], ALU.mult, [pbw, b_rope], [b_t2[ti]])
        if i < 8:
            hk, mp = i // 2, i % 2
            TT(P, "pool", qT[:, hk, h * NCH:(h + 1) * NCH, mp, :],
               t1[:, ti, :].rearrange("p (c q) -> p c q", q=64), t2[:, ti, :].rearrange("p (c q) -> p c q", q=64),
               ALU.add, [b_t1[ti], b_t2[ti]], [b_q[i][h]])
        else:
            hk = i - 8
            TT(P, "pool", kdup[:, hk, 128 + c0:128 + c1], t1[:, ti, :], t2[:, ti, :], ALU.add,
               [b_t1[ti], b_t2[ti]], [b_k[hk][h]])
            if T.last and h == NH - 1:
                TT(P, "dve", kf[:, hk, :], t1[:, ti, HC - NEW:HC], t2[:, ti, HC - NEW:HC], ALU.add,
                   [b_t1[ti], b_t2[ti]], [b_kf])

    pend_r = None
    cnt_r = 0
    for i in range(12):
        wm, wmb = getw(k, 1, ("A", 2 * i))
        for h in range(NH):
            c0, c1 = h * HC, (h + 1) * HC
            psa, pba = bank(k)
            for kc in range(KC):
                MM(P, psa[:, 0:HC], wm[:, kc * 128:(kc + 1) * 128], k.hT[:, kc, c0:c1], kc == 0, kc == KC - 1,
                   [wmb, k.b_hT[kc][h]], [pba])
            ai = cnt_r % 3
            ti = cnt_r % 3
            cnt_r += 1
            bias = k.vecs[:, V_BQK + 2 * i:V_BQK + 2 * i + 1]
            ACT(P, abf[:, ai, :], psa[:, 0:HC], AF.Identity, [pba, k.b_vecs], [b_abf[ai]], bias=bias)
            STT(P, t1[:, ti, :], psa[:, 0:HC], bias, cosT[:, c0:c1], ALU.add, ALU.mult, [pba, b_rope, k.b_vecs], [b_t1[ti]])
            if pend_r is not None:
                rope_finish(pend_r)
            pend_r = (i, h, ai, ti)
    rope_finish(pend_r)

    if sub < 4:
        return
    wv, wvb = getw(k, 2, ("V",))
    for j in range(NCK):
        h = (j * 64) // 512
        ps, pb = bank(k)
        for kc in range(KC):
            MM(P, ps[0:64, 0:256], k.hT[:, kc, j * 64:(j + 1) * 64], wv[:, kc * 256:(kc + 1) * 256], kc == 0, kc == KC - 1,
               [wvb, k.b_hT[kc][h]], [pb])
        TT(P, "dve", Vb[0:64, 2 + j, :], ps[0:64, 0:256], k.bvbc[0:64, :], ALU.add, [pb, k.b_const], [b_V[2 + j]])
        nlast = NEW // 64
        if T.last and j >= NCK - nlast:
            jj = j - (NCK - nlast)
            TT(P, "dve", vf[0:64, jj, :], ps[0:64, 0:256], k.bvbc[0:64, :], ALU.add, [pb, k.b_const], [b_vf])
            r0 = 128 - NEW + 64 * jj
            dst = k.vp[T.idx, r0:r0 + 64, :] if T.kind == "p" else k.vs[T.idx, r0:r0 + 64, :]
            DMA(P, "act", dst, vf[0:64, jj, :], [b_vf], [])

    if sub < 5:
        return
    if T.last:
        ps, pb = bank(k)
        for hk in range(4):
            TR(P, ps[0:NEW, hk * 64:(hk + 1) * 64], kf[0:64, hk, :], k.identf[0:64, 0:64], [b_kf, k.b_const], [pb])
        CP(P, "dve", kout[0:NEW, :], ps[0:NEW, 0:256], [pb], [b_kout])
        dst = k.kp[T.idx, :, :] if T.kind == "p" else k.ks[T.idx, 64:128, :]
        DMA(P, "act", dst, kout[0:NEW, :], [b_kout], [])

    if sub < 6:
        return
    items = [(c, hk) for c in range(NCK) for hk in range(4)]

    def kks_of(c):
        return [kk for kk in range(3) if hist or (c + kk) >= 2]

    def phase1(idx):
        c, hk = items[idx]
        h = (c * 64) // 512
        kks = kks_of(c)
        k0, k1 = kks[0], kks[-1] + 1
        pi = idx % 2
        for e in range(2):
            psS, pbS = bank(k)
            for kk in kks:
                bc = c + kk
                rdk = [b_kh] if bc < 2 else [b_k[hk][((bc - 2) * 64) // 512]]
                MM(P, psS[0:64, kk * 128:(kk + 1) * 128],
                   kdup[64 * e:64 * e + 64, hk, 64 * bc:64 * bc + 64], qT[64 * e:64 * e + 64, hk, c, :, :],
                   True, True, rdk + [b_q[2 * hk][h], b_q[2 * hk + 1][h]], [pbS])
            ACT(P, pt[0:64, pi, e, k0:k1, :], psS[0:64, k0 * 128:k1 * 128].rearrange("p (a b) -> p a b", b=128),
                AF.Exp, [pbS], [b_pt[pi]], scale=SCALE)

    def phase2(idx):
        c, hk = items[idx]
        h = (c * 64) // 512
        kks = kks_of(c)
        pi = idx % 2
        psO, pbO = bank(k)
        for grp in range(2):
            for e in range(2):
                for n, kk in enumerate(kks):
                    bc = c + kk
                    if grp == 0:
                        lhs = Vb[0:64, bc, 64 * hk:64 * hk + 64]
                        rd = [b_V[bc]]
                    else:
                        lhs = k.onesb[0:64, 0:64]
                        rd = [k.b_const]
                    MM(P, psO[64 * e:64 * e + 64, grp * 128:(grp + 1) * 128], lhs, pt[0:64, pi, e, kk, :],
                       n == 0, n == len(kks) - 1, rd + [b_pt[pi]], [pbO])
        TT(P, "dve", dn[:, pi, :], psO[:, 128:256], k.expsink[:, 2 * hk:2 * hk + 2, :].rearrange("p a b -> p (a b)"),
           ALU.add, [pbO, k.b_const], [b_dn[pi]])
        RECIP(P, dn[:, pi, :], dn[:, pi, :], [b_dn[pi]], [b_dn[pi]])
        TT(P, "dve", oT[:, 2 * hk:2 * hk + 2, 64 * c:64 * c + 64], psO[:, 0:128].rearrange("p (a b) -> p a b", b=64),
           dn[:, pi, :].rearrange("p (a b) -> p a b", b=64), ALU.mult, [pbO, b_dn[pi]], [b_oT[2 * hk][h], b_oT[2 * hk + 1][h]])

    phase1(0)
    for idx in range(len(items)):
        if idx + 1 < len(items):
            phase1(idx + 1)
        phase2(idx)

    if sub < 7:
        return
    if T.kind == "p" and not T.last:
        CP(P, "pool", k.khist[:, :, :], kdup[:, :, TT_:TT_ + 128], [b_k[hk][NH - 1] for hk in range(4)], [k.b_khist])
        CP(P, "pool", k.vhist[0:64, :, :], Vb[0:64, NCK:NCK + 2, :], [b_V[NCK], b_V[NCK + 1]], [k.b_vhist])

    for m in range(KC):
        w, wb = getw(k, 1, ("O", m))
        for h in range(NH):
            c0, c1 = h * HC, (h + 1) * HC
            ps, pb = bank(k)
            for kc in range(KC):
                MM(P, ps[:, 0:HC], w[:, kc * 128:(kc + 1) * 128], oT[:, kc, c0:c1], kc == 0, kc == KC - 1,
                   [wb, b_oT[kc][h]], [pb])
            post_evac(k, T, m, h, ps[:, 0:HC], [pb], k.vecs[:, V_BO + m:V_BO + m + 1])
    post_finish(k, T, 1)


def ffn_stage(k, T, l):
    P = k.P
    TT_, NH, HC = T.TT, T.NH, T.HC
    prenorm(k, T, l * 4 + 2)
    arena_switch(k)
    o = 0
    gT, o = carve(k, o, [NPAIR, TT_], BF16)
    acc, o = carve(k, o, [8, HC], F32)
    gl, o = carve(k, o, [2, HC], F32)
    b_gT = nb(k, NPAIR, NH)
    b_acc = nb(k, 8)
    b_gl = nb(k, 2)
    tp0 = k.tpar[l]
    if T.first:
        wb_ = [k.b_tails[l][u][tp0] for u in range(NUP)]
        if T.kind == "p":
            MEMSET(P, "dve", k.tails[:, l, :, :, :], 0.0, [k.b_tails[l][u][i_] for u in range(NUP) for i_ in range(3)])
        else:
            for s_ in range(2):
                for t_ in range(2):
                    DMA(P, "pool", k.tails[:, l, :, tp0, :].rearrange("p (j s) t -> p j s t", s=2)[:, :, s_, t_],
                        k.cconv[l, T.idx, t_, s_ * DFF:(s_ + 1) * DFF].rearrange("(j p) -> p j", p=128), [], wb_,
                        allow_slow_non_contiguous=True)
    gi = 0
    for j in range(NPAIR):
        for s in range(2):
            u = 2 * j + s
            w, wb = getw(k, 1, ("U", l, u))
            cw = V_CW + (l * NUP + u) * 3
            cb = k.vecs[:, V_CB + l * NUP + u:V_CB + l * NUP + u + 1]
            w0, w1, w2 = (k.vecs[:, cw + t:cw + t + 1] for t in range(3))
            for h in range(NH):
                tpa = (tp0 + h) % 3
                tpn = (tpa + 1) % 3
                tl = k.tails[:, l, u, tpa, :]
                btl = k.b_tails[l][u][tpa]
                tln = k.tails[:, l, u, tpn, :]
                btln = k.b_tails[l][u][tpn]
                c0, c1 = h * HC, (h + 1) * HC
                ps, pb = bank(k)
                for kc in range(KC):
                    MM(P, ps[:, 0:HC], w[:, kc * 128:(kc + 1) * 128], k.hT[:, kc, c0:c1], kc == 0, kc == KC - 1,
                       [wb, k.b_hT[kc][h]], [pb])
                ai = (j % 2) * 4 + s * 2 + h
                a = acc[:, ai, :]
                ba = b_acc[ai]
                CP(P, "act", tln, ps[:, HC - 2:HC], [pb], [btln])
                ACT(P, a, ps[:, 0:HC], AF.Identity, [pb, k.b_vecs], [ba], bias=cb, scale=w2)
                STT(P, a[:, 1:HC], ps[:, 0:HC - 1], w1, a[:, 1:HC], ALU.mult, ALU.add, [pb, ba, k.b_vecs], [ba])
                STT(P, a[:, 2:HC], ps[:, 0:HC - 2], w0, a[:, 2:HC], ALU.mult, ALU.add, [pb, ba, k.b_vecs], [ba])
                STT(P, a[:, 0:2], tl[:, 0:2], w0, a[:, 0:2], ALU.mult, ALU.add, [btl, ba, k.b_vecs], [ba])
                STT(P, a[:, 0:1], tl[:, 1:2], w1, a[:, 0:1], ALU.mult, ALU.add, [btl, ba, k.b_vecs], [ba])
        for h in range(NH):
            c0, c1 = h * HC, (h + 1) * HC
            ag = (j % 2) * 4 + h
            av = (j % 2) * 4 + 2 + h
            g = gi % 2
            gi += 1
            ACT(P, gl[:, g, :], acc[:, ag, :], AF.Gelu_apprx_tanh, [b_acc[ag]], [b_gl[g]])
            TT(P, "pool", gT[:, j, c0:c1], gl[:, g, :], acc[:, av, :], ALU.mult, [b_gl[g], b_acc[av]], [b_gT[j][h]])
    k.tpar[l] = (tp0 + NH) % 3
    if T.last:
        tpf = k.tpar[l]
        dst = k.cvp if T.kind == "p" else k.cvs
        for s_ in range(2):
            for t_ in range(2):
                DMA(P, "act", dst[l, T.idx, t_, s_ * DFF:(s_ + 1) * DFF].rearrange("(j p) -> p j", p=128),
                    k.tails[:, l, :, tpf, :].rearrange("p (j s) t -> p j s t", s=2)[:, :, s_, t_],
                    [k.b_tails[l][u][tpf] for u in range(NUP)], [], allow_slow_non_contiguous=True)
    for m in range(KC):
        w, wb = getw(k, 2, ("D", l, m))
        for h in range(NH):
            c0, c1 = h * HC, (h + 1) * HC
            ps, pb = bank(k)
            for kc in range(NPAIR):
                MM(P, ps[:, 0:HC], w[:, kc * 128:(kc + 1) * 128], gT[:, kc, c0:c1], kc == 0, kc == NPAIR - 1,
                   [wb, b_gT[kc][h]], [pb])
            post_evac(k, T, m, h, ps[:, 0:HC], [pb], None)
    post_finish(k, T, l * 4 + 3)


def tiles_of(k):
    TTP = min(1024, k.SEQ)
    tl = []
    for b in range(k.NP):
        n = k.SEQ // TTP
        for i in range(n):
            tl.append(Tile("p", b, i * TTP, TTP, i == 0, i == n - 1, i * TTP))
    for b in range(k.NS):
        tl.append(Tile("s", b, 0, k.DEC, True, True, k.SEQ))
    return tl


def program(k):
    P = k.P
    setup(k)
    if k.nlayers >= 2:
        ssm_setup(k)
    import os
    st = os.environ.get("KSTAGES", "xaf")
    for T in tiles_of(k):
        load_x(k, T)
        if "a" in st:
            attn_stage(k, T)
        if "f" in st:
            ffn_stage(k, T, 0)
        if k.nlayers >= 2:
            ssm_stage(k, T)
            ffn_stage(k, T, 1)
        store_y(k, T)
    P.fence()
    if not k.record:
        P.emit()


def make_nc(NP, SEQ, NS, DEC, nlayers=2):
    k1 = build(NP, SEQ, NS, DEC, nlayers)
    program(k1)
    k2 = build(NP, SEQ, NS, DEC, nlayers, wplan=k1.wplan)
    program(k2)
    return k2.nc


def _chunk_w(W):
    K_ = W.shape[0]
    return np.ascontiguousarray(W.reshape(K_ // 128, 128, W.shape[1]).transpose(1, 0, 2).reshape(128, -1))


def _swap_idx():
    idx = np.arange(128)
    d = idx % 64
    part = np.where(d < 8, idx + 8, np.where(d < 16, idx - 8, idx))
    return part


def host_consts(inp, SEQ, DEC):
    f32 = np.float32
    w_qkv = np.asarray(inp["w_qkv"][0], f32)
    b_qkv = np.asarray(inp["b_qkv"][0], f32)
    sw = _swap_idx()
    wA = np.zeros((24, 128, 1024), f32)
    vecs = np.zeros((128, V_N), f32)
    for i in range(12):
        if i < 8:
            cols = np.arange(128 * i, 128 * i + 128)
        else:
            hk = i - 8
            c = np.arange(1024 + 64 * hk, 1024 + 64 * hk + 64)
            cols = np.concatenate([c, c])
        wA[2 * i] = _chunk_w(w_qkv[:, cols])
        wA[2 * i + 1] = _chunk_w(w_qkv[:, cols[sw]])
        vecs[:, V_BQK + 2 * i] = b_qkv[cols]
        vecs[:, V_BQK + 2 * i + 1] = b_qkv[cols[sw]]
    wv = w_qkv[:, 1280:1536]
    wV = np.ascontiguousarray(wv.reshape(8, 128, 256).transpose(1, 0, 2).reshape(128, 2048))
    w_o = np.asarray(inp["w_o"][0], f32)
    wO = np.stack([_chunk_w(w_o[:, 128 * m:128 * m + 128]) for m in range(8)])
    w_up = np.asarray(inp["w_up"], f32)
    w_dn = np.asarray(inp["w_down"], f32)
    wUP = np.zeros((2, NUP, 128, 1024), f32)
    wDN = np.zeros((2, 8, 128, DFF), f32)
    conv_w = np.asarray(inp["conv_w"], f32)
    conv_b = np.asarray(inp["conv_b"], f32)
    for l in range(2):
        for u in range(NUP):
            cb = (u % 2) * DFF + (u // 2) * 128
            wUP[l, u] = _chunk_w(w_up[l][:, cb:cb + 128])
            vecs[:, V_CB + l * NUP + u] = conv_b[l, cb:cb + 128]
            for t in range(3):
                vecs[:, V_CW + (l * NUP + u) * 3 + t] = conv_w[l, t, cb:cb + 128]
        for m in range(8):
            wDN[l, m] = _chunk_w(w_dn[l][:, 128 * m:128 * m + 128])
    w_glu = np.asarray(inp["w_glu"][0], f32)
    b_glu = np.asarray(inp["b_glu"][0], f32)
    wGL = np.zeros((16, 128, 1024), f32)
    for m in range(8):
        for s in range(2):
            c0 = s * 1024 + 128 * m
            wGL[2 * m + s] = _chunk_w(w_glu[:, c0:c0 + 128])
            vecs[:, V_BGL + 2 * m + s] = b_glu[c0:c0 + 128]
    gains = [inp["g_pre_mix"][0], inp["g_post_mix"][0], inp["g_pre_ffn"][0], inp["g_post_ffn"][0],
             inp["g_pre_mix"][1], inp["g_post_mix"][1], inp["g_pre_ffn"][1], inp["g_post_ffn"][1]]
    for v, g in enumerate(gains):
        vecs[:, V_G + v * 8:V_G + v * 8 + 8] = np.asarray(g, f32).reshape(8, 128).T
    vecs[:, V_BO:V_BO + 8] = np.asarray(inp["b_o"][0], f32).reshape(8, 128).T
    vecs[:, V_SD:V_SD + 8] = np.asarray(inp["ssm_d"][0], f32).reshape(8, 128).T
    vecs[:, V_EPS] = EPS
    bvbc = np.tile(b_qkv[1280:1536][None, :], (128, 1)).astype(f32)
    sinks = np.asarray(inp["attn_sinks"][0], f32)
    sinkarr = np.zeros((128, 8), f32)
    for p in range(128):
        for hk in range(4):
            for mp in range(2):
                sinkarr[p, hk * 2 + mp] = sinks[4 * hk + 2 * mp + p // 64]
    half = 8
    inv_freq = np.power(f32(500000.0), -np.arange(half, dtype=f32) * f32(2.0 / 16)).astype(f32)
    pos = np.concatenate([np.arange(SEQ), PAST + np.arange(DEC)]).astype(f32)
    ang = (pos[:, None] * inv_freq[None, :]).astype(f32)
    cs, sn = np.cos(ang).astype(f32), np.sin(ang).astype(f32)
    rope = np.zeros((2, 128, SEQ + DEC), f32)
    rope[0] = 1.0
    for p in range(128):
        d = p % 64
        if d < 8:
            rope[0, p] = cs[:, d]
            rope[1, p] = -sn[:, d]
        elif d < 16:
            rope[0, p] = cs[:, d - 8]
            rope[1, p] = sn[:, d - 8]
    permsw = np.zeros((128, 128), f32)
    permsw[sw, np.arange(128)] = 1.0
    r_ = np.arange(128) // 16
    mask = (r_[None, :] >= r_[:, None]).astype(f32)
    lam = np.stack([np.asarray(inp["ssm_lam_re"][0], f32).T, np.asarray(inp["ssm_lam_im"][0], f32).T,
                    np.tile(np.asarray(inp["ssm_log_dt"][0], f32)[None, :], (64, 1))]).astype(f32)
    bc = np.stack([np.asarray(inp["ssm_b_re"][0], f32).transpose(1, 0, 2).reshape(64, 1024),
                   np.asarray(inp["ssm_b_im"][0], f32).transpose(1, 0, 2).reshape(64, 1024),
                   np.asarray(inp["ssm_c_re"][0], f32).transpose(2, 0, 1).reshape(64, 1024),
                   np.asarray(inp["ssm_c_im"][0], f32).transpose(2, 0, 1).reshape(64, 1024)]).astype(f32)
    return dict(wA=wA, wV=wV, wO=wO, wUP=wUP, wDN=wDN, wGL=wGL, vecs=vecs, bvbc=bvbc, sinkarr=sinkarr,
                ident=np.eye(128, dtype=f32), rope=rope, ssmmask=mask, permsw=permsw, ssmlam=np.ascontiguousarray(lam),
                ssmbc=np.ascontiguousarray(bc))


def run(inp, n_cores, nlayers=2):
    f32 = np.float32
    xp = np.asarray(inp["x_prompt"], f32)
    xs = np.asarray(inp["x_sample"], f32)
    B, SEQ, _ = xp.shape
    BS, DEC, _ = xs.shape
    NP, NS = B // n_cores, BS // n_cores
    consts = host_consts(inp, SEQ, DEC)
    ck = np.asarray(inp["cache_k"][0], f32).reshape(BS, 128, 256)
    cv = np.asarray(inp["cache_v"][0], f32).reshape(BS, 128, 256)
    sre = np.asarray(inp["state_ssm_re"][0], f32)
    sim = np.asarray(inp["state_ssm_im"][0], f32)
    cconv = np.asarray(inp["cache_conv"], f32)
    in_maps = []
    for c in range(n_cores):
        m = dict(consts)
        m["xp"] = np.ascontiguousarray(xp[c * NP:(c + 1) * NP])
        m["xs"] = np.ascontiguousarray(xs[c * NS:(c + 1) * NS])
        m["ck"] = np.ascontiguousarray(ck[c * NS:(c + 1) * NS])
        m["cv"] = np.ascontiguousarray(cv[c * NS:(c + 1) * NS])
        m["sre"] = np.ascontiguousarray(sre[c * NS:(c + 1) * NS])
        m["sim"] = np.ascontiguousarray(sim[c * NS:(c + 1) * NS])
        m["cconv"] = np.ascontiguousarray(cconv[:, c * NS:(c + 1) * NS])
        in_maps.append(m)
    nc = make_nc(NP, SEQ, NS, DEC, nlayers)
    res = run_bass_kernel_spmd(nc, in_maps, core_ids=list(range(n_cores)))
    R = res.results

    def cat(name, axis=0):
        return np.concatenate([np.asarray(r[name]) for r in R], axis=axis)
    y_p = cat("yp")
    y_s = cat("ys")
    k_p = cat("kp").reshape(1, B, 128, 4, 64)
    v_p = cat("vp").reshape(1, B, 128, 4, 64)
    k_s = cat("ks").reshape(1, BS, 128, 4, 64)
    v_s = cat("vs").reshape(1, BS, 128, 4, 64)
    re_p = cat("rep")[None]
    im_p = cat("imp")[None]
    re_s = cat("res")[None]
    im_s = cat("ims")[None]
    cv_p = cat("cvp", 1)
    cv_s = cat("cvs", 1)
    return tuple(np.ascontiguousarray(a, dtype=f32) for a in
                 (y_p, y_s, k_p, v_p, k_s, v_s, re_p, im_p, re_s, im_s, cv_p, cv_s))


def kernel(**inputs):
    return run(inputs, N_CORES, 2)


def ssm_setup(k):
    P = k.P
    arena_switch(k)
    TWO_PI = float(2.0 * np.pi)
    o = 0

    def cv(shape, dt=F32):
        nonlocal o
        ap, o = carve(k, o, shape, dt)
        return ap
    LR, LI, DT = cv([64]), cv([64]), cv([64])
    LRDT, LIDT = cv([64]), cv([64])
    ANG = cv([64, 8])
    MPt, MNt, MPn = cv([64, 8]), cv([64, 8]), cv([64, 8])
    CS, SN = cv([64, 8]), cv([64, 8])
    Fq, Nf = cv([64, 8]), cv([64, 8])
    Ni = cv([64, 8], I32)
    CR, CI, A8R, A8I = cv([64]), cv([64]), cv([64]), cv([64])
    w1, w2, w3 = cv([64]), cv([64]), cv([64])
    Bq = [cv([8, 16]) for _ in range(4)]
    BBR, BBI = cv([8, 16]), cv([8, 16])
    FR, FI = cv([8, 8, 16]), cv([8, 8, 16])
    GR, GI = cv([8, 8, 16]), cv([8, 8, 16])
    ER, EI = cv([8, 8, 16]), cv([8, 8, 16])
    T1, T2 = cv([8, 8, 16]), cv([8, 8, 16])
    stw = cv([8, 256], BF16)
    gst = cv([8, 2, 128], BF16)
    mask = cv([128])
    b = nb(k)
    bm = nb(k)
    bst = nb(k)
    bg = nb(k)
    Q = slice(0, 64)
    DMA(P, "act", LR[Q], k.lam_d[0], [], [b])
    DMA(P, "act", LI[Q], k.lam_d[1], [], [b])
    DMA(P, "act", DT[Q], k.lam_d[2], [], [b])
    DMA(P, "act", mask, k.mask_d[:, :], [], [bm])
    ACT(P, DT[Q], DT[Q], AF.Exp, [b], [b])
    TT(P, "dve", LRDT[Q], LR[Q], DT[Q], ALU.mult, [b], [b])
    TT(P, "dve", LIDT[Q], LI[Q], DT[Q], ALU.mult, [b], [b])
    for kk in range(1, 9):
        TS(P, "dve", ANG[Q, :, kk - 1], LIDT[Q], float(kk), ALU.mult, [b], [b])
        ACT(P, MPt[Q, :, kk - 1], LRDT[Q], AF.Exp, [b], [b], scale=float(kk))
        ACT(P, MNt[Q, :, kk - 1], LRDT[Q], AF.Exp, [b], [b], scale=float(-kk))
    TS(P, "dve", MPn[Q], MPt[Q], -1.0, ALU.mult, [b], [b])
    for (dst, shift) in ((SN, 0.0), (CS, 0.25)):
        TS(P, "dve", Fq[Q], ANG[Q], 1.0 / TWO_PI, ALU.mult, [b], [b], s2=shift, op1=ALU.add)
        CP(P, "dve", Ni[Q], Fq[Q], [b], [b])
        CP(P, "dve", Nf[Q], Ni[Q], [b], [b])
        TT(P, "dve", Fq[Q], Fq[Q], Nf[Q], ALU.subtract, [b], [b])
        TS(P, "dve", Nf[Q], Fq[Q], 0.5, ALU.is_gt, [b], [b])
        TT(P, "dve", Fq[Q], Fq[Q], Nf[Q], ALU.subtract, [b], [b])
        TS(P, "dve", Nf[Q], Fq[Q], -0.5, ALU.is_lt, [b], [b])
        TT(P, "dve", Fq[Q], Fq[Q], Nf[Q], ALU.add, [b], [b])
        TS(P, "dve", Fq[Q], Fq[Q], -0.49999997, ALU.max, [b], [b], s2=0.49999997, op1=ALU.min)
        ACT(P, dst[Q], Fq[Q], AF.Sin, [b], [b], scale=TWO_PI)
    TT(P, "dve", w1[Q], MPt[Q, :, 0], CS[Q, :, 0], ALU.mult, [b], [b])
    TS(P, "dve", w1[Q], w1[Q], -1.0, ALU.add, [b], [b])
    TT(P, "dve", w2[Q], MPt[Q, :, 0], SN[Q, :, 0], ALU.mult, [b], [b])
    TT(P, "dve", w3[Q], LR[Q], LR[Q], ALU.mult, [b], [b])
    TT(P, "dve", CR[Q], LI[Q], LI[Q], ALU.mult, [b], [b])
    TT(P, "dve", w3[Q], w3[Q], CR[Q], ALU.add, [b], [b])
    RECIP(P, w3[Q], w3[Q], [b], [b])
    TT(P, "dve", CR[Q], w1[Q], LR[Q], ALU.mult, [b], [b])
    TT(P, "dve", CI[Q], w2[Q], LI[Q], ALU.mult, [b], [b])
    TT(P, "dve", CR[Q], CR[Q], CI[Q], ALU.add, [b], [b])
    TT(P, "dve", CR[Q], CR[Q], w3[Q], ALU.mult, [b], [b])
    TT(P, "dve", CI[Q], w2[Q], LR[Q], ALU.mult, [b], [b])
    TT(P, "dve", w1[Q], w1[Q], LI[Q], ALU.mult, [b], [b])
    TT(P, "dve", CI[Q], CI[Q], w1[Q], ALU.subtract, [b], [b])
    TT(P, "dve", CI[Q], CI[Q], w3[Q], ALU.mult, [b], [b])
    TT(P, "dve", A8R[Q], MPt[Q, :, 7], CS[Q, :, 7], ALU.mult, [b], [b])
    TT(P, "dve", A8I[Q], MPt[Q, :, 7], SN[Q, :, 7], ALU.mult, [b], [b])
    DMA(P, "act", k.a8d[0], A8R[Q], [b], [k.b_a8d])
    DMA(P, "act", k.a8d[1], A8I[Q], [b], [k.b_a8d])
    ba = Buf()
    for gl in range(2):
        for (comp, sl) in ((0, (0, 0)), (0, (0, 1)), (1, (1, 1)), (1, (1, 0))):
            src = k.a8d[comp].rearrange("p (gh gl) -> gl p gh", gl=2)[gl]
            DMA(P, "act", k.a12[64 * gl:64 * gl + 64, sl[0], sl[1], :], src, [k.b_a8d], [ba], allow_slow_non_contiguous=True)
    TS(P, "dve", k.a12[:, 1, 0, :], k.a12[:, 1, 0, :], -1.0, ALU.mult, [ba], [k.b_const])
    for n_, srcap in ((2, CS[Q, :, 7]), (3, SN[Q, :, 7]), (4, MPt[Q, :, 7])):
        CP(P, "dve", w1[Q], srcap, [b], [b])
        DMA(P, "act", k.a8d[n_], w1[Q], [b], [k.b_a8d])
    for n_ in range(3):
        for gl in range(2):
            src = k.a8d[2 + n_].rearrange("p (gh gl) -> gl p gh", gl=2)[gl]
            DMA(P, "act", k.rot1[64 * gl:64 * gl + 64, n_, :], src, [k.b_a8d], [ba], allow_slow_non_contiguous=True)
    RC = cv([32, 65])
    RS = cv([32, 65])
    U1 = T1.rearrange("p a b c -> p (a b c)").rearrange("p (a b) -> p a b", b=32)
    U2 = T2.rearrange("p a b c -> p (a b c)").rearrange("p (a b) -> p a b", b=32)
    MEMSET(P, "dve", RC[:, :, 0], 1.0, [b])
    MEMSET(P, "dve", RS[:, :, 0], 0.0, [b])
    CP(P, "dve", RC[:, :, 1], k.rot1[:, 0, :], [ba, b], [b])
    CP(P, "dve", RS[:, :, 1], k.rot1[:, 1, :], [ba, b], [b])
    span = 1
    while span < 64:
        n = span
        cS = RC[:, :, span:span + 1].broadcast_to([128, 32, n])
        sS = RS[:, :, span:span + 1].broadcast_to([128, 32, n])
        a_c, a_s = RC[:, :, 1:1 + n], RS[:, :, 1:1 + n]
        u1, u2 = U1[:, :, 0:n], U2[:, :, 0:n]
        TT(P, "dve", u1, a_c, cS, ALU.mult, [b], [b])
        TT(P, "dve", u2, a_s, sS, ALU.mult, [b], [b])
        TT(P, "dve", RC[:, :, span + 1:span + 1 + n], u1, u2, ALU.subtract, [b], [b])
        TT(P, "dve", u1, a_c, sS, ALU.mult, [b], [b])
        TT(P, "dve", u2, a_s, cS, ALU.mult, [b], [b])
        TT(P, "dve", RS[:, :, span + 1:span + 1 + n], u1, u2, ALU.add, [b], [b])
        span *= 2
    DMA(P, "act", k.rtab[0], RC.rearrange("p a b -> p (a b)"), [b], [k.b_rtab])
    DMA(P, "act", k.rtab[1], RS.rearrange("p a b -> p (a b)"), [b], [k.b_rtab])

    def bj(ap2, g0):
        return ap2[Q, g0:g0 + 8].unsqueeze(2).unsqueeze(3).broadcast_to([64, 8, 8, 16])

    def bk(tab, g0):
        return tab[Q, g0:g0 + 8, :].unsqueeze(3).broadcast_to([64, 8, 8, 16])

    def br(ap3):
        return ap3[Q].unsqueeze(2).broadcast_to([64, 8, 8, 16])
    btab = nb(k).inherit([b])
    bcin = nb(k)
    bGc = nb(k).inherit([b])
    T3 = RC.rearrange("p a b -> p (a b)")[:, 0:1024].rearrange("p (a b c) -> p a b c", b=8, c=16)
    T4 = RS.rearrange("p a b -> p (a b)")[:, 0:1024].rearrange("p (a b c) -> p a b c", b=8, c=16)
    for gb in range(8):
        g0 = gb * 8
        for n in range(4):
            DMA(P, "act", Bq[n][Q], k.bc_d[n][:, g0 * 16:(g0 + 8) * 16].rearrange("p (g j) -> p g j", j=16), [],
                [b] if n < 2 else [bcin])
        c8 = CR[Q, g0:g0 + 8].unsqueeze(2).broadcast_to([64, 8, 16])
        i8 = CI[Q, g0:g0 + 8].unsqueeze(2).broadcast_to([64, 8, 16])
        t1s, t2s = T1[Q, :, 0, :], T2[Q, :, 0, :]
        TT(P, "dve", t1s, Bq[0][Q], c8, ALU.mult, [b], [b])
        TT(P, "dve", t2s, Bq[1][Q], i8, ALU.mult, [b], [b])
        TT(P, "dve", BBR[Q], t1s, t2s, ALU.subtract, [b], [b])
        TT(P, "dve", t1s, Bq[1][Q], c8, ALU.mult, [b], [b])
        TT(P, "dve", t2s, Bq[0][Q], i8, ALU.mult, [b], [b])
        TT(P, "dve", BBI[Q], t1s, t2s, ALU.add, [b], [b])
        TT(P, "dve", T1[Q], bk(CS, g0), br(BBR), ALU.mult, [b], [b])
        TT(P, "dve", T2[Q], bk(SN, g0), br(BBI), ALU.mult, [b], [b])
        TT(P, "dve", T1[Q], T1[Q], T2[Q], ALU.add, [b], [b])
        TT(P, "dve", FR[Q], T1[Q], bk(MNt, g0), ALU.mult, [b], [b])
        TT(P, "dve", T1[Q], bk(CS, g0), br(BBI), ALU.mult, [b], [b])
        TT(P, "dve", T2[Q], bk(SN, g0), br(BBR), ALU.mult, [b], [b])
        TT(P, "dve", T1[Q], T1[Q], T2[Q], ALU.subtract, [b], [b])
        TT(P, "dve", FI[Q], T1[Q], bk(MNt, g0), ALU.mult, [b], [b])
        rdG = [btab, bcin, bGc]
        TT(P, "pool", T3[Q], bk(CS, g0), br(Bq[2]), ALU.mult, rdG, [bGc])
        TT(P, "pool", T4[Q], bk(SN, g0), br(Bq[3]), ALU.mult, rdG, [bGc])
        TT(P, "pool", T3[Q], T3[Q], T4[Q], ALU.subtract, rdG, [bGc])
        TT(P, "pool", GR[Q], T3[Q], bk(MPt, g0), ALU.mult, rdG, [bGc])
        TT(P, "pool", T3[Q], bk(SN, g0), br(Bq[2]), ALU.mult, rdG, [bGc])
        TT(P, "pool", T4[Q], bk(CS, g0), br(Bq[3]), ALU.mult, rdG, [bGc])
        TT(P, "pool", T3[Q], T3[Q], T4[Q], ALU.add, rdG, [bGc])
        TT(P, "pool", GI[Q], T3[Q], bk(MPn, g0), ALU.mult, rdG, [bGc])
        TT(P, "dve", T1[Q], FR[Q], bj(A8R, g0), ALU.mult, [b], [b])
        TT(P, "dve", T2[Q], FI[Q], bj(A8I, g0), ALU.mult, [b], [b])
        TT(P, "dve", ER[Q], T1[Q], T2[Q], ALU.subtract, [b], [b])
        TT(P, "dve", T1[Q], FI[Q], bj(A8R, g0), ALU.mult, [b], [b])
        TT(P, "dve", T2[Q], FR[Q], bj(A8I, g0), ALU.mult, [b], [b])
        TT(P, "dve", EI[Q], T1[Q], T2[Q], ALU.add, [b], [b])
        for gl in range(8):
            ps, pb = bank(k)
            fr = FR[Q, gl, :, :].rearrange("p a b -> p (a b)")
            fi = FI[Q, gl, :, :].rearrange("p a b -> p (a b)")
            gr = GR[Q, gl, :, :].rearrange("p a b -> p (a b)")
            gi = GI[Q, gl, :, :].rearrange("p a b -> p (a b)")
            MM(P, ps[:, 0:128], fr, gr, True, False, [b, bGc], [pb])
            MM(P, ps[:, 0:128], fi, gi, False, True, [b, bGc], [pb])
            TT(P, "dve", stw[:, gl, 0:128], ps[:, 0:128], mask, ALU.mult, [pb, bm], [bst])
            ps2, pb2 = bank(k)
            TR(P, ps2[:, 0:64], ER[Q, gl, :, :].rearrange("p a b -> p (a b)"), k.identf[0:64, 0:64], [b, k.b_const], [pb2])
            TR(P, ps2[:, 64:128], EI[Q, gl, :, :].rearrange("p a b -> p (a b)"), k.identf[0:64, 0:64], [b, k.b_const], [pb2])
            CP(P, "act", stw[:, gl, 128:256], ps2[:, 0:128], [pb2], [bst])
        DMA(P, "act", k.ssmW[g0:g0 + 8].rearrange("g p c -> p g c"), stw, [bst], k.b_ssmW[g0:g0 + 8])
        CP(P, "act", gst[Q, :, 0, :], GR[Q].rearrange("p g a b -> p g (a b)"), [bGc], [bg])
        CP(P, "act", gst[Q, :, 1, :], GI[Q].rearrange("p g a b -> p g (a b)"), [bGc], [bg])
        DMA(P, "act", k.ssmG[g0 // 2:g0 // 2 + 4].rearrange("gp (gl p) c -> p (gp gl) c", gl=2),
            gst[Q].rearrange("p g a b -> p g (a b)"), [bg], k.b_ssmG[g0 // 2:g0 // 2 + 4])


def ssm_stage(k, T):
    P = k.P
    TT_, NH, HC = T.TT, T.NH, T.HC
    SUBT = min(TT_, 512)
    NSUB = TT_ // SUBT
    NC = SUBT // 8
    prenorm(k, T, 4)
    arena_switch(k)
    o = 0
    tm, o = carve(k, o, [8192], BF16)
    Usup, o = carve(k, o, [NG, NC], BF16)
    Gs, o = carve(k, o, [2, 32, NC + 1], F32)
    RCt, o = carve(k, o, [32, 65], F32)
    RSt, o = carve(k, o, [32, 65], F32)
    T1, o = carve(k, o, [16, NC], F32)
    T2, o = carve(k, o, [16, NC], F32)
    tB, o = carve(k, o, [2, 4, NC], F32)
    sct, o = carve(k, o, [2, 2, 32], F32)
    Hbf, o = carve(k, o, [2, 32, NC], BF16)
    Ysb, o = carve(k, o, [2, 2, 4, NC], BF16)
    tz, o = carve(k, o, [2, 512], F32)
    tm_g = tm.rearrange("p (g r j) -> p g r j", r=8, j=16)
    tm_r = tm.rearrange("p (r g j) -> p r g j", g=NG, j=16)
    b_tz = nb(k, 2)
    tzi = 0
    b_rt = nb(k)
    DMA(P, "pool", RCt.rearrange("p a b -> p (a b)"), k.rtab[0], [k.b_rtab], [b_rt])
    DMA(P, "pool", RSt.rearrange("p a b -> p (a b)"), k.rtab[1], [k.b_rtab], [b_rt])
    b_T = nb(k, 2)
    b_tB = nb(k, 2)
    b_G = nb(k, 8, 2)
    b_R = nb(k, 2, 32)
    allG = [b_G[g_][c_] for g_ in range(8) for c_ in range(2)] + [b_R[c_][g_] for c_ in range(2) for g_ in range(32)]
    if T.first:
        if T.kind == "p":
            MEMSET(P, "dve", k.hstate[:, :, :], 0.0, [k.b_hstate])
        else:
            for comp, src in ((0, k.sre), (1, k.sim), (2, k.sre)):
                for gl in range(2):
                    DMA(P, "pool", k.hstate[64 * gl:64 * gl + 64, comp, :],
                        src[T.idx].rearrange("(gh gl) p -> gl p gh", gl=2)[gl], [], [k.b_hstate],
                        allow_slow_non_contiguous=True)
    b_tmU = nb(k)
    b_U = nb(k, 8)
    b_S = nb(k)
    b_H = nb(k)
    b_tmY = nb(k, 8)
    b_Y = nb(k, 2, 2)
    for sbi in range(NSUB):
        cb = sbi * SUBT
        hh = cb // 512
        b_tmU.inherit(b_tmY)
        for r in range(8):
            ps, pb = bank(k)
            psb = ps[:, :].bitcast(BF16)
            for kc in range(KC):
                TR(P, psb[0:NC, kc * 128:(kc + 1) * 128], k.hT[:, kc, cb + r:cb + SUBT:8], k.identb[:, :],
                   [k.b_hT[kc][hh], k.b_const], [pb])
            CP(P, "act" if r % 2 == 0 else "dve", tm_g[0:NC, :, r, :], psb[0:NC, 0:1024].rearrange("p (g j) -> p g j", j=16),
               [pb], [b_tmU])
        for gb in range(8):
            ps, pb = bank(k)
            psb = ps[:, :].bitcast(BF16)
            for gl in range(8):
                g = gb * 8 + gl
                TR(P, psb[:, gl * NC:(gl + 1) * NC], tm_g[0:NC, g, :, :].rearrange("p r j -> p (r j)"), k.identb[0:NC, 0:NC],
                   [b_tmU, k.b_const], [pb])
            CP(P, "act" if gb % 2 == 0 else "dve", Usup[:, gb * 8:(gb + 1) * 8, :],
               psb[:, 0:8 * NC].rearrange("p (g c) -> p g c", c=NC), [pb], [b_U[gb]])
        for gb in range(8):
            wsw, wswb = getw(k, 2, ("SW", gb * 8))
            wv = wsw.rearrange("p (g c) -> p g c", c=256)
            psr, pbr = bank(k)
            psi, pbi = bank(k)
            for gl in range(8):
                g = gb * 8 + gl
                g2, gpl = g % 2, gl // 2
                MM(P, psr[64 * g2:64 * g2 + 64, gpl * NC:(gpl + 1) * NC], wv[:, gl, 128:192], Usup[:, g, :], True, True,
                   [wswb, b_U[gb]], [pbr])
                MM(P, psi[64 * g2:64 * g2 + 64, gpl * NC:(gpl + 1) * NC], wv[:, gl, 192:256], Usup[:, g, :], True, True,
                   [wswb, b_U[gb]], [pbi])
            gp0 = gb * 4
            rc = RCt[:, gp0:gp0 + 4, 1:NC + 1]
            rs = RSt[:, gp0:gp0 + 4, 1:NC + 1]
            prv = psr[:, 0:4 * NC].rearrange("p (g c) -> p g c", c=NC)
            piv = psi[:, 0:4 * NC].rearrange("p (g c) -> p g c", c=NC)
            gre = Gs[:, 0, gp0:gp0 + 4, 1:NC + 1]
            gim = Gs[:, 1, gp0:gp0 + 4, 1:NC + 1]
            TT(P, "dve", gre, prv, rc, ALU.mult, [pbr, b_rt], [b_G[gb][0]])
            TT(P, "dve", tB[:, 0, :, :], piv, rs, ALU.mult, [pbi, b_rt], [b_tB[0]])
            TT(P, "pool", gre, gre, tB[:, 0, :, :], ALU.add, [b_tB[0]], [b_G[gb][0]])
            TT(P, "dve", gim, piv, rc, ALU.mult, [pbi, b_rt], [b_G[gb][1]])
            TT(P, "dve", tB[:, 1, :, :], prv, rs, ALU.mult, [pbr, b_rt], [b_tB[1]])
            TT(P, "pool", gim, gim, tB[:, 1, :, :], ALU.subtract, [b_tB[1]], [b_G[gb][1]])
        CP(P, "dve", Gs[:, :, :, 0], k.hstate[:, 0:2, :], [k.b_hstate], [b_S])
        for comp in range(2):
            for gp in range(32):
                row = Gs[:, comp, gp, 1:NC + 1]
                m8 = k.rot1[:, 2, gp:gp + 1].broadcast_to([128, NC])
                ini = Gs[:, comp, gp, 0:1]
                P.op("dve", (lambda e, row=row, m8=m8, ini=ini: e.tensor_tensor_scan(out=row, data0=m8, data1=row, initial=ini,
                                                                                      op0=ALU.mult, op1=ALU.add)),
                     [b_S, k.b_const, b_G[gp // 4][comp]], [b_R[comp][gp]])
        for hf in range(2):
            gs_ = slice(16 * hf, 16 * hf + 16)
            rc = RCt[:, gs_, 0:NC]
            rs = RSt[:, gs_, 0:NC]
            gre = Gs[:, 0, gs_, 0:NC]
            gim = Gs[:, 1, gs_, 0:NC]
            TT(P, "dve", T1, rc, gre, ALU.mult, [b_rt] + allG, [b_T[0]])
            TT(P, "pool", T2, rs, gim, ALU.mult, [b_rt] + allG, [b_T[1]])
            TT(P, "dve", Hbf[:, 0, gs_, :], T1, T2, ALU.subtract, [b_T[0], b_T[1]], [b_H])
            TT(P, "dve", T1, rc, gim, ALU.mult, [b_rt] + allG, [b_T[0]])
            TT(P, "pool", T2, rs, gre, ALU.mult, [b_rt] + allG, [b_T[1]])
            TT(P, "dve", Hbf[:, 1, gs_, :], T1, T2, ALU.add, [b_T[0], b_T[1]], [b_H])
        rcl, rsl = RCt[:, :, NC], RSt[:, :, NC]
        grl, gil = Gs[:, 0, :, NC], Gs[:, 1, :, NC]
        u1, u2 = sct[:, 0, 0, :], sct[:, 0, 1, :]
        TT(P, "dve", u1, rcl, grl, ALU.mult, [b_rt] + allG, [b_S])
        TT(P, "dve", u2, rsl, gil, ALU.mult, [b_rt] + allG, [b_S])
        TT(P, "dve", k.hstate[:, 0, :], u1, u2, ALU.subtract, [b_S], [k.b_hstate])
        TT(P, "dve", u1, rcl, gil, ALU.mult, [b_rt] + allG, [b_S])
        TT(P, "dve", u2, rsl, grl, ALU.mult, [b_rt] + allG, [b_S])
        TT(P, "dve", k.hstate[:, 1, :], u1, u2, ALU.add, [b_S], [k.b_hstate])
        for bb in b_tmY:
            bb.inherit([b_tmU])
        yi = 0
        for gb in range(8):
            wsw, wswb = getw(k, 2, ("SW", gb * 8))
            wv = wsw.rearrange("p (g c) -> p g c", c=256)
            wsg, wsgb = getw(k, 1, ("SG", gb * 4))
            gv = wsg.rearrange("p (g c) -> p g c", c=256)
            yb_ = yi % 2
            yi += 1
            for par in range(2):
                ps, pb = bank(k)
                for q in range(4):
                    gl = 2 * q + par
                    g = gb * 8 + gl
                    gp = g // 2
                    out = ps[:, q * NC:(q + 1) * NC]
                    MM(P, out, wv[:, gl, 0:128], Usup[:, g, :], True, False, [wswb, b_U[gb]], [pb])
                    MM(P, out, gv[64 * par:64 * par + 64, q, 0:128], Hbf[64 * par:64 * par + 64, 0, gp, :], False, False,
                       [wsgb, b_H], [pb])
                    MM(P, out, gv[64 * par:64 * par + 64, q, 128:256], Hbf[64 * par:64 * par + 64, 1, gp, :], False, True,
                       [wsgb, b_H], [pb])
                CP(P, "act" if par == 0 else "dve", Ysb[:, yb_, par, :, :], ps[:, 0:4 * NC].rearrange("p (g c) -> p g c", c=NC),
                   [pb], [b_Y[yb_][par]])
            ps, pb = bank(k)
            psb = ps[:, :].bitcast(BF16)
            for gl in range(8):
                TR(P, psb[0:NC, gl * 128:(gl + 1) * 128], Ysb[:, yb_, gl % 2, gl // 2, :], k.identb[:, :],
                   [b_Y[yb_][gl % 2], k.b_const], [pb])
            CP(P, "dve" if gb % 2 == 0 else "act", tm_r[0:NC, :, gb * 8:(gb + 1) * 8, :],
               psb[0:NC, 0:1024].rearrange("p (g r i) -> p r g i", r=8, i=16), [pb], [b_tmY[gb]])
        for kc in range(KC):
            ps, pb = bank(k)
            psb = ps[:, :].bitcast(BF16)
            for r in range(8):
                TR(P, psb[:, r * NC:(r + 1) * NC], tm_r[0:NC, r, kc * 8:(kc + 1) * 8, :].rearrange("p g i -> p (g i)"),
                   k.identb[0:NC, 0:NC], [b_tmY[kc], k.b_const], [pb])
            ti = tzi % 2
            tzi += 1
            hv = k.hT[:, kc, cb:cb + SUBT].rearrange("p (c r) -> p r c", r=8)
            tv = tz[:, ti, 0:SUBT].rearrange("p (r c) -> p r c", c=NC)
            STT(P, tv, hv, k.vecs[:, V_SD + kc:V_SD + kc + 1], psb[:, 0:SUBT].rearrange("p (r c) -> p r c", c=NC),
                ALU.mult, ALU.add, [k.b_hT[kc][hh], pb, k.b_vecs], [b_tz[ti]])
            ACT(P, hv, tv, AF.Gelu_apprx_tanh, [b_tz[ti]], [k.b_hT[kc][hh]])
    if T.last:
        dre, dim_ = (k.rep, k.imp) if T.kind == "p" else (k.res, k.ims)
        for comp, dst in ((0, dre), (1, dim_)):
            for gl in range(2):
                DMA(P, "act", dst[T.idx].rearrange("(gh gl) p -> gl p gh", gl=2)[gl], k.hstate[64 * gl:64 * gl + 64, comp, :],
                    [k.b_hstate], [], allow_slow_non_contiguous=True)
    b_g = b_tz
    gi = 0
    for m in range(KC):
        wa, wab = getw(k, 1, ("G", 2 * m))
        wg, wgb = getw(k, 1, ("G", 2 * m + 1))
        for h in range(NH):
            c0, c1 = h * HC, (h + 1) * HC
            psa, pba = bank(k)
            psg, pbg = bank(k)
            for kc in range(KC):
                MM(P, psa[:, 0:HC], wa[:, kc * 128:(kc + 1) * 128], k.hT[:, kc, c0:c1], kc == 0, kc == KC - 1,
                   [wab, k.b_hT[kc][h]], [pba])
            for kc in range(KC):
                MM(P, psg[:, 0:HC], wg[:, kc * 128:(kc + 1) * 128], k.hT[:, kc, c0:c1], kc == 0, kc == KC - 1,
                   [wgb, k.b_hT[kc][h]], [pbg])
            ti = gi % 2
            gi += 1
            ACT(P, tz[:, ti, 0:HC], psg[:, 0:HC], AF.Sigmoid, [pbg, k.b_vecs], [b_g[ti]],
                bias=k.vecs[:, V_BGL + 2 * m + 1:V_BGL + 2 * m + 2])
            STT(P, tz[:, ti, 0:HC], psa[:, 0:HC], k.vecs[:, V_BGL + 2 * m:V_BGL + 2 * m + 1], tz[:, ti, 0:HC], ALU.add, ALU.mult,
                [pba, b_g[ti], k.b_vecs], [b_g[ti]])
            post_evac(k, T, m, h, tz[:, ti, 0:HC], [b_g[ti]], None)
    post_finish(k, T, 5)
```

```python
import numpy as np
import concourse.bass as bass
import concourse.mybir as mybir
from concourse.bass_utils import run_bass_kernel_spmd

F32 = mybir.dt.float32
BF16 = mybir.dt.bfloat16
I32 = mybir.dt.int32
U8 = mybir.dt.uint8
AF = mybir.ActivationFunctionType
ALU = mybir.AluOpType

D = 1024
KC = 8
DFF = 2816
NPAIR = 22
NUP = 44
NG = 64
PAST = 1024
EPS = 1e-6
SCALE = 0.125
N_CORES = 8

V_G = 0
V_BQK = 64
V_BO = 88
V_BGL = 96
V_SD = 112
V_CB = 120
V_CW = 208
V_EPS = 472
V_N = 473


class Buf:
    __slots__ = ("w", "r", "excl")

    def __init__(self, excl=False):
        self.w = {}
        self.r = {}
        self.excl = excl

    def inherit(self, others):
        for o in others:
            for s, v in o.w.items():
                if self.w.get(s, 0) < v:
                    self.w[s] = v
            for s, v in o.r.items():
                if self.r.get(s, 0) < v:
                    self.r[s] = v
        return self


def bufs(*shape):
    if len(shape) == 1:
        return [Buf() for _ in range(shape[0])]
    return [bufs(*shape[1:]) for _ in range(shape[0])]


class Prog:
    ENGS = ("pe", "act", "dve", "pool", "sp")

    def __init__(self, nc):
        self.nc = nc
        self.eobj = dict(pe=nc.tensor, act=nc.scalar, dve=nc.vector, pool=nc.gpsimd, sp=nc.sync)
        self.sems = []
        self.stream = {e: [] for e in self.ENGS}
        self.esem = {e: self._newsem("e_" + e) for e in self.ENGS}
        self.cnt = {e: 0 for e in self.ENGS}
        self.waited = {e: {} for e in self.ENGS}
        self.NDS = 8
        self.dsem = {q: [self._newsem("d_%s%d" % (q, i)) for i in range(self.NDS)] for q in ("sp", "pool", "act")}
        self.dcnt = {q: 0 for q in self.dsem}
        self.nbank = 0

    def _newsem(self, name):
        self.sems.append(self.nc.alloc_semaphore(name))
        return len(self.sems) - 1

    def _collect(self, eng, reads, writes, extra=None):
        deps = dict(extra) if extra else {}
        own = self.esem.get(eng, -1)
        for b in reads:
            for s, v in b.w.items():
                if deps.get(s, 0) < v:
                    deps[s] = v
            if b.excl:
                for s, v in b.r.items():
                    if s != own and deps.get(s, 0) < v:
                        deps[s] = v
        for b in writes:
            for s, v in b.w.items():
                if deps.get(s, 0) < v:
                    deps[s] = v
            for s, v in b.r.items():
                if deps.get(s, 0) < v:
                    deps[s] = v
        wd = self.waited[eng]
        waits = []
        pe_self = self.esem["pe"] if eng == "pe" else -1
        for s, v in deps.items():
            if s == pe_self:
                continue
            if wd.get(s, 0) < v:
                wd[s] = v
                waits.append((s, v))
        return waits

    def _mark(self, tok, reads, writes):
        s, v = tok
        for b in reads:
            b.r[s] = v
        for b in writes:
            b.w = {s: v}
            b.r = {}

    @staticmethod
    def _excl(reads, writes):
        if any(b.excl for b in reads):
            writes = list(writes) + [b for b in reads if b.excl]
            reads = [b for b in reads if not b.excl]
        return reads, writes

    def op(self, eng, fn, reads=(), writes=()):
        waits = self._collect(eng, reads, writes)
        self.cnt[eng] += 1
        tok = (self.esem[eng], self.cnt[eng])
        self.stream[eng].append((waits, fn, (tok[0], 1)))
        self._mark(tok, reads, writes)
        return tok

    def dma(self, q, fn, reads=(), writes=()):
        k = self.dcnt[q]
        self.dcnt[q] += 1
        slot = k % self.NDS
        use = k // self.NDS
        s = self.dsem[q][slot]
        extra = {s: 16 * use} if use > 0 else None
        waits = self._collect(q, reads, writes, extra)
        tok = (s, 16 * (use + 1))
        self.stream[q].append((waits, fn, (s, 16)))
        self._mark(tok, reads, writes)
        return tok

    def all_tokens(self):
        tgt = {}
        for e in self.ENGS:
            if self.cnt[e] > 0:
                tgt[self.esem[e]] = self.cnt[e]
        for q in self.dsem:
            k = self.dcnt[q]
            for slot in range(self.NDS):
                uses = (k - slot + self.NDS - 1) // self.NDS
                if uses > 0:
                    tgt[self.dsem[q][slot]] = 16 * uses
        return tgt

    def fence(self, engs=None):
        tgt = self.all_tokens()
        for e in (engs or self.ENGS):
            wd = self.waited[e]
            waits = []
            for s, v in tgt.items():
                if e == "pe" and s == self.esem["pe"]:
                    continue
                if wd.get(s, 0) < v:
                    wd[s] = v
                    waits.append((s, v))
            if waits:
                self.stream[e].append((waits, None, None))

    def emit(self):
        nc = self.nc
        with nc.Block() as block:
            def mk(e):
                def body(_):
                    eo = self.eobj[e]
                    for waits, fn, inc in self.stream[e]:
                        if fn is None:
                            for s, v in waits:
                                eo.wait_ge(self.sems[s], v)
                            continue
                        for s, v in waits[:-1]:
                            eo.wait_ge(self.sems[s], v)
                        ins = fn(eo)
                        if waits:
                            s, v = waits[-1]
                            ins.wait_op(self.sems[s], v, "sem-ge")
                        ins.then_inc(self.sems[inc[0]], inc[1])
                return body
            block.tensor(mk("pe"))
            block.scalar(mk("act"))
            block.vector(mk("dve"))
            block.gpsimd(mk("pool"))
            block.sync(mk("sp"))


def MM(P, out, lhsT, rhs, start, stop, reads, writes):
    return P.op("pe", lambda e: e.matmul(out, lhsT=lhsT, rhs=rhs, start=start, stop=stop), reads, writes)


def TR(P, out, in_, ident, reads, writes):
    return P.op("pe", lambda e: e.transpose(out, in_, ident), reads, writes)


def ACT(P, out, in_, func, reads, writes, bias=None, scale=None):
    kw = {}
    if bias is not None:
        kw["bias"] = bias
    if scale is not None:
        kw["scale"] = scale
    return P.op("act", lambda e: e.activation(out=out, in_=in_, func=func, **kw), reads, writes)


def TT(P, eng, out, in0, in1, op, reads, writes):
    return P.op(eng, lambda e: e.tensor_tensor(out=out, in0=in0, in1=in1, op=op), reads, writes)


def STT(P, out, in0, scalar, in1, op0, op1, reads, writes):
    return P.op("dve", lambda e: e.scalar_tensor_tensor(out=out, in0=in0, scalar=scalar, in1=in1, op0=op0, op1=op1),
                reads, writes)


def TS(P, eng, out, in0, s1, op0, reads, writes, s2=None, op1=None):
    if op1 is None:
        return P.op(eng, lambda e: e.tensor_scalar(out=out, in0=in0, scalar1=s1, scalar2=None, op0=op0), reads, writes)
    return P.op(eng, lambda e: e.tensor_scalar(out=out, in0=in0, scalar1=s1, scalar2=s2, op0=op0, op1=op1), reads, writes)


def CP(P, eng, out, in_, reads, writes):
    if eng == "act":
        return P.op("act", lambda e: e.activation(out=out, in_=in_, func=AF.Copy), reads, writes)
    return P.op(eng, lambda e: e.tensor_copy(out=out, in_=in_), reads, writes)


def DMA(P, q, out, in_, reads, writes, **kw):
    return P.dma(q, lambda e: e.dma_start(out=out, in_=in_, **kw), reads, writes)


class K:
    pass


def build(NP, SEQ, NS, DEC, nlayers=2, dbg=None, wplan=None):
    nc = bass.Bass("TRN2", target_bir_lowering=False)
    P = Prog(nc)
    k = K()
    k.nc, k.P, k.NP, k.SEQ, k.NS, k.DEC = nc, P, NP, SEQ, NS, DEC
    TTP = min(1024, SEQ)
    TMAX = max(TTP, DEC)
    k.TMAX = TMAX

    def din(name, shape, dt=F32):
        return nc.dram_tensor(name, list(shape), dt, kind="ExternalInput").ap()

    def dout(name, shape, dt=F32):
        return nc.dram_tensor(name, list(shape), dt, kind="ExternalOutput").ap()

    def dscr(name, shape, dt=BF16):
        return nc.dram_tensor(name, list(shape), dt, kind="Internal").ap()

    k.xp = din("xp", [NP, SEQ, D])
    k.xs = din("xs", [NS, DEC, D])
    k.ck = din("ck", [NS, 128, 256])
    k.cv = din("cv", [NS, 128, 256])
    k.sre = din("sre", [NS, 64, 64])
    k.sim = din("sim", [NS, 64, 64])
    k.cconv = din("cconv", [2, NS, 2, 2 * DFF])
    k.wA = din("wA", [24, 128, 1024])
    k.wV = din("wV", [128, 2048])
    k.wO = din("wO", [8, 128, 1024])
    k.wUP = din("wUP", [2, NUP, 128, 1024])
    k.wDN = din("wDN", [2, 8, 128, DFF])
    k.wGL = din("wGL", [16, 128, 1024])
    k.vecs_d = din("vecs", [128, V_N])
    k.bv_d = din("bvbc", [128, 256])
    k.sink_d = din("sinkarr", [128, 8])
    k.ident_d = din("ident", [128, 128])
    k.rope_d = din("rope", [2, 128, SEQ + DEC])
    k.mask_d = din("ssmmask", [128, 128])
    k.perm_d = din("permsw", [128, 128])
    k.lam_d = din("ssmlam", [3, 64, 64])
    k.bc_d = din("ssmbc", [4, 64, 64 * 16])

    k.yp = dout("yp", [NP, SEQ, D])
    k.ys = dout("ys", [NS, DEC, D])
    k.kp = dout("kp", [NP, 128, 256])
    k.vp = dout("vp", [NP, 128, 256])
    k.ks = dout("ks", [NS, 128, 256])
    k.vs = dout("vs", [NS, 128, 256])
    k.rep = dout("rep", [NP, 64, 64])
    k.imp = dout("imp", [NP, 64, 64])
    k.res = dout("res", [NS, 64, 64])
    k.ims = dout("ims", [NS, 64, 64])
    k.cvp = dout("cvp", [2, NP, 2, 2 * DFF])
    k.cvs = dout("cvs", [2, NS, 2, 2 * DFF])

    k.wAb = dscr("wAb", [24, 128, 1024])
    k.wVb = dscr("wVb", [128, 2048])
    k.wOb = dscr("wOb", [8, 128, 1024])
    k.wUPb = dscr("wUPb", [2, NUP, 128, 1024])
    k.wDNb = dscr("wDNb", [2, 8, 128, DFF])
    k.wGLb = dscr("wGLb", [16, 128, 1024])
    k.ssmW = dscr("ssmW", [NG, 128, 256])
    k.ssmG = dscr("ssmG", [NG // 2, 128, 256])
    k.a8d = dscr("a8d", [5, 64, 64], F32)
    k.rtab = dscr("rtab", [2, 128, 32 * 65], F32)
    k.b_rtab = Buf()
    k.b_wA = bufs(24)
    k.b_wV = Buf()
    k.b_wO = bufs(8)
    k.b_wUP = bufs(2, NUP)
    k.b_wDN = bufs(2, 8)
    k.b_wGL = bufs(16)
    k.b_ssmW = bufs(NG)
    k.b_ssmG = bufs(NG // 2)
    k.b_a8d = Buf()

    def sb(name, shape, dt):
        return nc.alloc_sbuf_tensor(name, list(shape), dt)

    k.xT = sb("xT", [128, KC, TMAX], F32)
    k.b_xT = bufs(KC, 2)
    k.hT = sb("hT", [128, KC, TMAX], BF16)
    k.b_hT = bufs(KC, 2)
    k.mT = sb("mT", [128, KC, TMAX], BF16)
    k.b_mT = bufs(KC, 2)
    k.rstd = sb("rstd", [128, TMAX], F32)
    k.b_rstd = bufs(2)
    k.sq = sb("sq", [128, 4, 512], BF16)
    k.b_sq = bufs(4)
    k.sqi = 0
    k.ring1 = sb("ring1", [128, 8, 1024], BF16)
    k.b_ring1 = bufs(8)
    k.r1i = 0
    k.ring2 = sb("ring2", [128, 3, DFF], BF16)
    k.b_ring2 = bufs(3)
    k.r2i = 0
    k.vecs = sb("vecs_s", [128, V_N], F32)
    k.b_vecs = Buf()
    k.identf = sb("identf", [128, 128], F32)
    k.identb = sb("identb", [128, 128], BF16)
    k.onesb = sb("onesb", [128, 128], BF16)
    k.permb = sb("permb", [128, 128], BF16)
    k.b_const = Buf()
    k.xsi = 0
    k.tails = sb("tails", [128, 2, NUP, 3, 2], F32)
    k.b_tails = bufs(2, NUP, 3)
    k.khist = sb("khist", [128, 4, 128], BF16)
    k.b_khist = Buf()
    k.vhist = sb("vhist", [128, 2, 256], BF16)
    k.b_vhist = Buf()
    k.expsink = sb("expsink", [128, 8, 64], F32)
    k.bvbc = sb("bvbc_s", [128, 256], F32)
    k.hstate = sb("hstate", [128, 3, 32], F32)
    k.b_hstate = Buf()
    k.a12 = sb("a12", [128, 2, 2, 32], F32)
    k.rot1 = sb("rot1", [128, 3, 32], F32)
    k.ARENA = 83 * 1024
    k.arena = sb("arena", [128, k.ARENA], U8)
    k.ps = [nc.alloc_psum_tensor("ps%d" % i, [128, 512], F32) for i in range(8)]
    k.b_ps = [Buf(excl=True) for _ in range(8)]
    k.dbg = dbg or {}
    k.arena_sum = Buf()
    k.cur_arena = []
    k.pend = []
    k.tpar = [0, 0]
    k.record = wplan is None
    k.wplan = wplan if wplan is not None else {1: [], 2: []}
    k.nlayers = nlayers
    k.tmpA = sb("tmpA", [128, 2, 512], F32)
    k.b_tmpA = bufs(2)
    k.tmpi = 0
    k.dbg_out = {}
    return k


def carve(k, off, shape, dt):
    esz = 4 if dt in (F32, I32) else 2
    n = int(np.prod(shape))
    nb = n * esz
    off = (off + 31) // 32 * 32
    assert off + nb <= k.ARENA, (off, nb, k.ARENA)
    ap = k.arena[:, off:off + nb].bitcast(dt)
    if len(shape) == 2:
        ap = ap.rearrange("p (a b) -> p a b", b=shape[1])
    elif len(shape) == 3:
        ap = ap.rearrange("p (a b c) -> p a b c", b=shape[1], c=shape[2])
    elif len(shape) == 4:
        ap = ap.rearrange("p (a b c d) -> p a b c d", b=shape[1], c=shape[2], d=shape[3])
    return ap, off + nb


def arena_switch(k):
    k.arena_sum.inherit(k.cur_arena)
    k.cur_arena = []
    import os
    if os.environ.get("KFENCE"):
        k.P.fence()


def nb(k, *shape):
    def mk():
        b = Buf().inherit([k.arena_sum])
        k.cur_arena.append(b)
        return b
    if len(shape) == 0:
        return mk()
    if len(shape) == 1:
        return [mk() for _ in range(shape[0])]
    return [nb(k, *shape[1:]) for _ in range(shape[0])]


def bank(k, n=6):
    i = k.P.nbank % n
    k.P.nbank += 1
    return k.ps[i], k.b_ps[i]


def RECIP(P, out, in_, reads, writes):
    return P.op("dve", lambda e: e.reciprocal(out=out, in_=in_), reads, writes)


def MEMSET(P, eng, ap, val, writes):
    return P.op(eng, lambda e: e.memset(ap, val), [], writes)


class Ring:
    def __init__(self, tensor, bufs_, nslot):
        self.tensor, self.bufs, self.nslot = tensor, bufs_, nslot
        self.issued = 0
        self.used = 0


def wsrc(k, key):
    n = key[0]
    if n == "A":
        return k.wAb[key[1]], [k.b_wA[key[1]]]
    if n == "V":
        return k.wVb[:, :], [k.b_wV]
    if n == "O":
        return k.wOb[key[1]], [k.b_wO[key[1]]]
    if n == "U":
        return k.wUPb[key[1], key[2]], [k.b_wUP[key[1]][key[2]]]
    if n == "D":
        return k.wDNb[key[1], key[2]], [k.b_wDN[key[1]][key[2]]]
    if n == "G":
        return k.wGLb[key[1]], [k.b_wGL[key[1]]]
    if n == "SW":
        g0 = key[1]
        return k.ssmW[g0:g0 + 8].rearrange("g p c -> p g c"), k.b_ssmW[g0:g0 + 8]
    if n == "SG":
        g0 = key[1]
        return k.ssmG[g0:g0 + 4].rearrange("g p c -> p g c"), k.b_ssmG[g0:g0 + 4]
    raise KeyError(key)


def _issue_w(k, ring, upto):
    R = k.rings[ring]
    lst = k.wplan[ring]
    upto = min(upto, len(lst) - 1)
    while R.issued <= upto:
        j = R.issued
        src, sb_ = wsrc(k, lst[j])
        s = j % R.nslot
        shp = src.shape
        fs = int(np.prod(shp[1:]))
        dst = R.tensor[:, s, 0:fs]
        if len(shp) == 3:
            dst = dst.rearrange("p (a b) -> p a b", b=shp[2])
        DMA(k.P, "sp", dst, src, sb_, [R.bufs[s]])
        R.issued += 1


def getw(k, ring, key):
    R = k.rings[ring]
    i = R.used
    R.used += 1
    if k.record:
        k.wplan[ring].append(key)
    else:
        assert k.wplan[ring][i] == key, (k.wplan[ring][i], key)
        _issue_w(k, ring, i + R.nslot - 2)
    s = i % R.nslot
    src, _ = wsrc(k, key)
    fs = int(np.prod(src.shape[1:]))
    return R.tensor[:, s, 0:fs], R.bufs[s]


class Tile:
    def __init__(self, kind, idx, t0, TT, first, last, pos0):
        self.kind, self.idx, self.t0, self.TT, self.first, self.last, self.pos0 = kind, idx, t0, TT, first, last, pos0
        self.NH = max(1, TT // 512)
        self.HC = min(TT, 512)
        self.BT = min(128, TT)
        self.NBLK = TT // self.BT


def setup(k):
    P = k.P
    nc = k.nc
    k.rings = {1: Ring(k.ring1, k.b_ring1, 8), 2: Ring(k.ring2, k.b_ring2, 3)}
    DMA(P, "act", k.vecs[:], k.vecs_d[:, :], [], [k.b_vecs])
    DMA(P, "act", k.identf[:], k.ident_d[:, :], [], [k.b_const])
    DMA(P, "act", k.bvbc[:], k.bv_d[:, :], [], [k.b_const])
    CP(P, "dve", k.identb[:], k.identf[:], [k.b_const], [k.b_const])
    MEMSET(P, "dve", k.onesb[:], 1.0, [k.b_const])
    pf, _o = carve(k, 2048, [128], F32)
    bpf = nb(k)
    DMA(P, "act", pf, k.perm_d[:, :], [], [bpf])
    CP(P, "dve", k.permb[:], pf, [bpf], [k.b_const])
    zt, o = carve(k, 0, [64], F32)
    sk, o = carve(k, o, [8], F32)
    bz = nb(k)
    MEMSET(P, "dve", zt, 0.0, [bz])
    DMA(P, "act", sk, k.sink_d[:, :], [], [bz])
    ACT(P, sk, sk, AF.Exp, [bz], [bz])
    for j in range(8):
        TS(P, "dve", k.expsink[:, j, :], zt, sk[:, j:j + 1], ALU.add, [bz], [k.b_const])
    def conv(dst, src, dbufs):
        DMA(P, "pool", dst, src, [], dbufs)
    for a in range(0, 24, 2):
        conv(k.wAb[a:a + 1], k.wA[a:a + 1], k.b_wA[a:a + 1])
    conv(k.wVb[:, :], k.wV[:, :], [k.b_wV])
    conv(k.wOb[:], k.wO[:], k.b_wO)
    for l in range(2):
        for a in range(0, NUP, 8):
            b = min(NUP, a + 8)
            conv(k.wUPb[l, a:b], k.wUP[l, a:b], k.b_wUP[l][a:b])
        for a in range(0, 8, 4):
            conv(k.wDNb[l, a:a + 4], k.wDN[l, a:a + 4], k.b_wDN[l][a:a + 4])
        if l == 0:
            for a in range(0, 16, 8):
                conv(k.wGLb[a:a + 8], k.wGL[a:a + 8], k.b_wGL[a:a + 8])


def load_x(k, T):
    P = k.P
    BT = T.BT
    arena_switch(k)
    k.xstage, _ = carve(k, 0, [2, 1024], F32)
    k.b_xstage = nb(k, 2)
    src = k.xp if T.kind == "p" else k.xs
    for j in range(T.NBLK):
        s = k.xsi % 2
        k.xsi += 1
        DMA(P, "pool", k.xstage[0:BT, s, :], src[T.idx, T.t0 + j * BT:T.t0 + (j + 1) * BT, :], [], [k.b_xstage[s]])
        h = (j * BT) // 512
        for hb in range(2):
            ps, pb = bank(k)
            for q in range(4):
                kc = hb * 4 + q
                TR(P, ps[:, q * BT:(q + 1) * BT], k.xstage[0:BT, s, kc * 128:(kc + 1) * 128], k.identf[0:BT, 0:BT],
                   [k.b_xstage[s], k.b_const], [pb])
            CP(P, "act", k.xT[:, hb * 4:hb * 4 + 4, j * BT:(j + 1) * BT],
               ps[:, 0:4 * BT].rearrange("p (q t) -> p q t", t=BT), [pb], [k.b_xT[hb * 4 + q][h] for q in range(4)])


def store_y(k, T):
    P = k.P
    BT = T.BT
    arena_switch(k)
    k.xstage, _ = carve(k, 0, [2, 1024], F32)
    k.b_xstage = nb(k, 2)
    dst = k.yp if T.kind == "p" else k.ys
    for j in range(T.NBLK):
        s = k.xsi % 2
        k.xsi += 1
        h = (j * BT) // 512
        for hb in range(2):
            ps, pb = bank(k)
            for q in range(4):
                kc = hb * 4 + q
                TR(P, ps[0:BT, q * 128:(q + 1) * 128], k.xT[:, kc, j * BT:(j + 1) * BT], k.identf[:, :],
                   [k.b_xT[kc][h], k.b_const], [pb])
            CP(P, "act", k.xstage[0:BT, s, hb * 512:(hb + 1) * 512], ps[0:BT, 0:512],
               [pb], [k.b_xstage[s]])
        DMA(P, "act", dst[T.idx, T.t0 + j * BT:T.t0 + (j + 1) * BT, :], k.xstage[0:BT, s, :], [k.b_xstage[s]], [])


def finish_rstd(k, T, h):
    P = k.P
    c0, c1 = h * T.HC, (h + 1) * T.HC
    ssb, ssbuf = k.ps[6 + h], k.b_ps[6 + h]
    ACT(P, k.rstd[:, c0:c1], ssb[:, 0:T.HC], AF.Sqrt, [ssbuf, k.b_vecs], [k.b_rstd[h]],
        bias=k.vecs[:, V_EPS:V_EPS + 1], scale=1.0 / D)
    RECIP(P, k.rstd[:, c0:c1], k.rstd[:, c0:c1], [k.b_rstd[h]], [k.b_rstd[h]])


def prenorm(k, T, vec):
    P = k.P
    HC = T.HC
    for h in range(T.NH):
        c0, c1 = h * HC, (h + 1) * HC
        for kc in range(KC):
            si = k.sqi % 4
            k.sqi += 1
            ACT(P, k.sq[:, si, 0:HC], k.xT[:, kc, c0:c1], AF.Square, [k.b_xT[kc][h]], [k.b_sq[si]])
            MM(P, k.ps[6 + h][:, 0:HC], k.onesb[:, :], k.sq[:, si, 0:HC], kc == 0, kc == KC - 1,
               [k.b_sq[si], k.b_const], [k.b_ps[6 + h]])
        finish_rstd(k, T, h)
        for kc in range(KC):
            g = k.vecs[:, V_G + vec * 8 + kc:V_G + vec * 8 + kc + 1]
            STT(P, k.hT[:, kc, c0:c1], k.xT[:, kc, c0:c1], g, k.rstd[:, c0:c1], ALU.mult, ALU.mult,
                [k.b_xT[kc][h], k.b_rstd[h], k.b_vecs], [k.b_hT[kc][h]])


def post_evac(k, T, m, h, src, src_bufs, bias):
    P = k.P
    HC = T.HC
    c0, c1 = h * HC, (h + 1) * HC
    rd = list(src_bufs) + [k.b_vecs]
    ACT(P, k.mT[:, m, c0:c1], src, AF.Identity, rd, [k.b_mT[m][h]], bias=bias)
    si = k.sqi % 4
    k.sqi += 1
    ACT(P, k.sq[:, si, 0:HC], src, AF.Square, rd, [k.b_sq[si]], bias=bias)
    k.pend.append((h, si, HC, m == 0, m == KC - 1))
    import os
    while len(k.pend) > int(os.environ.get("KPEND", "2")):
        flush_ss(k, 1)


def flush_ss(k, n=None):
    P = k.P
    while k.pend and (n is None or n > 0):
        h, si, HC, st, sp = k.pend.pop(0)
        MM(P, k.ps[6 + h][:, 0:HC], k.onesb[:, :], k.sq[:, si, 0:HC], st, sp, [k.b_sq[si], k.b_const], [k.b_ps[6 + h]])
        if n is not None:
            n -= 1


def post_finish(k, T, vec):
    P = k.P
    HC = T.HC
    flush_ss(k)
    for h in range(T.NH):
        c0, c1 = h * HC, (h + 1) * HC
        finish_rstd(k, T, h)
        for m in range(KC):
            g = k.vecs[:, V_G + vec * 8 + m:V_G + vec * 8 + m + 1]
            ti = k.tmpi % 2
            k.tmpi += 1
            STT(P, k.tmpA[:, ti, 0:HC], k.mT[:, m, c0:c1], g, k.rstd[:, c0:c1], ALU.mult, ALU.mult,
                [k.b_mT[m][h], k.b_rstd[h], k.b_vecs], [k.b_tmpA[ti]])
            TT(P, "pool", k.xT[:, m, c0:c1], k.xT[:, m, c0:c1], k.tmpA[:, ti, 0:HC], ALU.add,
               [k.b_tmpA[ti], k.b_xT[m][h]], [k.b_xT[m][h]])


def attn_stage(k, T):
    P = k.P
    TT_, NH, HC, BT, NBLK = T.TT, T.NH, T.HC, T.BT, T.NBLK
    NCK = TT_ // 64
    NCH = HC // 64
    NEW = min(128, TT_)
    import os
    sub = int(os.environ.get("KSUB", "9"))
    prenorm(k, T, 0)
    if sub < 2:
        return
    arena_switch(k)
    o = 0
    qT, o = carve(k, o, [4, NCK, 2, 64], BF16)
    kdup, o = carve(k, o, [4, 128 + TT_], BF16)
    Vb, o = carve(k, o, [NCK + 2, 256], BF16)
    oT, o = carve(k, o, [KC, TT_], BF16)
    cosT, o = carve(k, o, [TT_], F32)
    sinT, o = carve(k, o, [TT_], F32)
    t1, o = carve(k, o, [3, HC], F32)
    t2, o = carve(k, o, [3, HC], F32)
    abf, o = carve(k, o, [3, HC], BF16)
    pt, o = carve(k, o, [2, 2, 3, 128], BF16)
    kf, o = carve(k, o, [4, NEW], F32)
    vf, o = carve(k, o, [2, 256], F32)
    dn, o = carve(k, o, [2, 128], F32)
    kcb, o = carve(k, o, [256], BF16)
    kout, o = carve(k, o, [256], F32)
    b_q = nb(k, 8, NH)
    b_k = nb(k, 4, NH)
    b_kh = nb(k)
    b_V = nb(k, NCK + 2)
    b_oT = nb(k, KC, NH)
    b_rope = nb(k)
    b_t1 = nb(k, 3)
    b_t2 = nb(k, 3)
    b_abf = nb(k, 3)
    b_pt = nb(k, 2)
    b_kf = nb(k)
    b_vf = nb(k)
    b_dn = nb(k, 2)
    b_kcb = nb(k)
    b_kout = nb(k)
    DMA(P, "pool", cosT, k.rope_d[0, :, T.pos0:T.pos0 + TT_], [], [b_rope])
    DMA(P, "pool", sinT, k.rope_d[1, :, T.pos0:T.pos0 + TT_], [], [b_rope])
    hist = not (T.kind == "p" and T.first)
    if T.kind == "s":
        DMA(P, "pool", kcb, k.ck[T.idx], [], [b_kcb])
        DMA(P, "pool", Vb[0:64, 0, :], k.cv[T.idx, 0:64, :], [], [b_V[0]])
        DMA(P, "pool", Vb[0:64, 1, :], k.cv[T.idx, 64:128, :], [], [b_V[1]])
        ps, pb = bank(k)
        psb = ps[:, :].bitcast(BF16)
        for hk in range(4):
            for e in range(2):
                TR(P, psb[64 * e:64 * e + 64, hk * 128:(hk + 1) * 128], kcb[:, hk * 64:(hk + 1) * 64], k.identb[:, :],
                   [b_kcb, k.b_const], [pb])
        CP(P, "dve", kdup[:, :, 0:128], psb[:, 0:512].rearrange("p (h t) -> p h t", t=128), [pb], [b_kh])
        DMA(P, "act", k.ks[T.idx, 0:64, :], k.ck[T.idx, 64:128, :], [], [])
        DMA(P, "act", k.vs[T.idx, 0:64, :], k.cv[T.idx, 64:128, :], [], [])
    elif hist:
        CP(P, "pool", kdup[:, :, 0:128], k.khist[:, :, :], [k.b_khist], [b_kh])
        CP(P, "pool", Vb[0:64, 0:2, :], k.vhist[0:64, :, :], [k.b_vhist], [b_V[0], b_V[1]])

    if sub < 3:
        return
    def rope_finish(item):
        i, h, ai, ti = item
        c0, c1 = h * HC, (h + 1) * HC
        psw, pbw = bank(k)
        MM(P, psw[:, 0:HC], k.permb[:, :], abf[:, ai, :], True, True, [b_abf[ai], k.b_const], [pbw])
        TT(P, "dve", t2[:, ti, :], psw[:, 0:HC], sinT[:, c0:c1], ALU.mult, [pbw, b_rope], [b_t2[ti]])
        if i < 8:
            hk, mp = i // 2, i % 2
            TT(P, "pool", qT[:, hk, h * NCH:(h + 1) * NCH, mp, :],
               t1[:, ti, :].rearrange("p (c q) -> p c q", q=64), t2[:, ti, :].rearrange("p (c q) -> p c q", q=64),
               ALU.add, [b_t1[ti], b_t2[ti]], [b_q[i][h]])
        else:
            hk = i - 8
            TT(P, "pool", kdup[:, hk, 128 + c0:128 + c1], t1[:, ti, :], t2[:, ti, :], ALU.add,
               [b_t1[ti], b_t2[ti]], [b_k[hk][h]])
            if T.last and h == NH - 1:
                TT(P, "dve", kf[:, hk, :], t1[:, ti, HC - NEW:HC], t2[:, ti, HC - NEW:HC], ALU.add,
                   [b_t1[ti], b_t2[ti]], [b_kf])

    pend_r = None
    cnt_r = 0
    for i in range(12):
        wm, wmb = getw(k, 1, ("A", 2 * i))
        for h in range(NH):
            c0, c1 = h * HC, (h + 1) * HC
            psa, pba = bank(k)
            for kc in range(KC):
                MM(P, psa[:, 0:HC], wm[:, kc * 128:(kc + 1) * 128], k.hT[:, kc, c0:c1], kc == 0, kc == KC - 1,
                   [wmb, k.b_hT[kc][h]], [pba])
            ai = cnt_r % 3
            ti = cnt_r % 3
            cnt_r += 1
            bias = k.vecs[:, V_BQK + 2 * i:V_BQK + 2 * i + 1]
            ACT(P, abf[:, ai, :], psa[:, 0:HC], AF.Identity, [pba, k.b_vecs], [b_abf[ai]], bias=bias)
            STT(P, t1[:, ti, :], psa[:, 0:HC], bias, cosT[:, c0:c1], ALU.add, ALU.mult, [pba, b_rope, k.b_vecs], [b_t1[ti]])
            if pend_r is not None:
                rope_finish(pend_r)
            pend_r = (i, h, ai, ti)
    rope_finish(pend_r)

    if sub < 4:
        return
    wv, wvb = getw(k, 2, ("V",))
    for j in range(NCK):
        h = (j * 64) // 512
        ps, pb = bank(k)
        for kc in range(KC):
            MM(P, ps[0:64, 0:256], k.hT[:, kc, j * 64:(j + 1) * 64], wv[:, kc * 256:(kc + 1) * 256], kc == 0, kc == KC - 1,
               [wvb, k.b_hT[kc][h]], [pb])
        TT(P, "dve", Vb[0:64, 2 + j, :], ps[0:64, 0:256], k.bvbc[0:64, :], ALU.add, [pb, k.b_const], [b_V[2 + j]])
        nlast = NEW // 64
        if T.last and j >= NCK - nlast:
            jj = j - (NCK - nlast)
            TT(P, "dve", vf[0:64, jj, :], ps[0:64, 0:256], k.bvbc[0:64, :], ALU.add, [pb, k.b_const], [b_vf])
            r0 = 128 - NEW + 64 * jj
            dst = k.vp[T.idx, r0:r0 + 64, :] if T.kind == "p" else k.vs[T.idx, r0:r0 + 64, :]
            DMA(P, "act", dst, vf[0:64, jj, :], [b_vf], [])

    if sub < 5:
        return
    if T.last:
        ps, pb = bank(k)
        for hk in range(4):
            TR(P, ps[0:NEW, hk * 64:(hk + 1) * 64], kf[0:64, hk, :], k.identf[0:64, 0:64], [b_kf, k.b_const], [pb])
        CP(P, "dve", kout[0:NEW, :], ps[0:NEW, 0:256], [pb], [b_kout])
        dst = k.kp[T.idx, :, :] if T.kind == "p" else k.ks[T.idx, 64:128, :]
        DMA(P, "act", dst, kout[0:NEW, :], [b_kout], [])

    if sub < 6:
        return
    items = [(c, hk) for c in range(NCK) for hk in range(4)]

    def kks_of(c):
        return [kk for kk in range(3) if hist or (c + kk) >= 2]

    def phase1(idx):
        c, hk = items[idx]
        h = (c * 64) // 512
        kks = kks_of(c)
        k0, k1 = kks[0], kks[-1] + 1
        pi = idx % 2
        for e in range(2):
            psS, pbS = bank(k)
            for kk in kks:
                bc = c + kk
                rdk = [b_kh] if bc < 2 else [b_k[hk][((bc - 2) * 64) // 512]]
                MM(P, psS[0:64, kk * 128:(kk + 1) * 128],
                   kdup[64 * e:64 * e + 64, hk, 64 * bc:64 * bc + 64], qT[64 * e:64 * e + 64, hk, c, :, :],
                   True, True, rdk + [b_q[2 * hk][h], b_q[2 * hk + 1][h]], [pbS])
            ACT(P, pt[0:64, pi, e, k0:k1, :], psS[0:64, k0 * 128:k1 * 128].rearrange("p (a b) -> p a b", b=128),
                AF.Exp, [pbS], [b_pt[pi]], scale=SCALE)

    def phase2(idx):
        c, hk = items[idx]
        h = (c * 64) // 512
        kks = kks_of(c)
        pi = idx % 2
        psO, pbO = bank(k)
        for grp in range(2):
            for e in range(2):
                for n, kk in enumerate(kks):
                    bc = c + kk
                    if grp == 0:
                        lhs = Vb[0:64, bc, 64 * hk:64 * hk + 64]
                        rd = [b_V[bc]]
                    else:
                        lhs = k.onesb[0:64, 0:64]
                        rd = [k.b_const]
                    MM(P, psO[64 * e:64 * e + 64, grp * 128:(grp + 1) * 128], lhs, pt[0:64, pi, e, kk, :],
                       n == 0, n == len(kks) - 1, rd + [b_pt[pi]], [pbO])
        TT(P, "dve", dn[:, pi, :], psO[:, 128:256], k.expsink[:, 2 * hk:2 * hk + 2, :].rearrange("p a b -> p (a b)"),
           ALU.add, [pbO, k.b_const], [b_dn[pi]])
        RECIP(P, dn[:, pi, :], dn[:, pi, :], [b_dn[pi]], [b_dn[pi]])
        TT(P, "dve", oT[:, 2 * hk:2 * hk + 2, 64 * c:64 * c + 64], psO[:, 0:128].rearrange("p (a b) -> p a b", b=64),
           dn[:, pi, :].rearrange("p (a b) -> p a b", b=64), ALU.mult, [pbO, b_dn[pi]], [b_oT[2 * hk][h], b_oT[2 * hk + 1][h]])

    phase1(0)
    for idx in range(len(items)):
        if idx + 1 < len(items):
            phase1(idx + 1)
        phase2(idx)

    if sub < 7:
        return
    if T.kind == "p" and not T.last:
        CP(P, "pool", k.khist[:, :, :], kdup[:, :, TT_:TT_ + 128], [b_k[hk][NH - 1] for hk in range(4)], [k.b_khist])
        CP(P, "pool", k.vhist[0:64, :, :], Vb[0:64, NCK:NCK + 2, :], [b_V[NCK], b_V[NCK + 1]], [k.b_vhist])

    for m in range(KC):
        w, wb = getw(k, 1, ("O", m))
        for h in range(NH):
            c0, c1 = h * HC, (h + 1) * HC
            ps, pb = bank(k)
            for kc in range(KC):
                MM(P, ps[:, 0:HC], w[:, kc * 128:(kc + 1) * 128], oT[:, kc, c0:c1], kc == 0, kc == KC - 1,
                   [wb, b_oT[kc][h]], [pb])
            post_evac(k, T, m, h, ps[:, 0:HC], [pb], k.vecs[:, V_BO + m:V_BO + m + 1])
    post_finish(k, T, 1)


def ffn_stage(k, T, l):
    P = k.P
    TT_, NH, HC = T.TT, T.NH, T.HC
    prenorm(k, T, l * 4 + 2)
    arena_switch(k)
    o = 0
    gT, o = carve(k, o, [NPAIR, TT_], BF16)
    acc, o = carve(k, o, [8, HC], F32)
    gl, o = carve(k, o, [2, HC], F32)
    b_gT = nb(k, NPAIR, NH)
    b_acc = nb(k, 8)
    b_gl = nb(k, 2)
    tp0 = k.tpar[l]
    if T.first:
        wb_ = [k.b_tails[l][u][tp0] for u in range(NUP)]
        if T.kind == "p":
            MEMSET(P, "dve", k.tails[:, l, :, :, :], 0.0, [k.b_tails[l][u][i_] for u in range(NUP) for i_ in range(3)])
        else:
            for s_ in range(2):
                for t_ in range(2):
                    DMA(P, "pool", k.tails[:, l, :, tp0, :].rearrange("p (j s) t -> p j s t", s=2)[:, :, s_, t_],
                        k.cconv[l, T.idx, t_, s_ * DFF:(s_ + 1) * DFF].rearrange("(j p) -> p j", p=128), [], wb_,
                        allow_slow_non_contiguous=True)
    gi = 0
    for j in range(NPAIR):
        for s in range(2):
            u = 2 * j + s
            w, wb = getw(k, 1, ("U", l, u))
            cw = V_CW + (l * NUP + u) * 3
            cb = k.vecs[:, V_CB + l * NUP + u:V_CB + l * NUP + u + 1]
            w0, w1, w2 = (k.vecs[:, cw + t:cw + t + 1] for t in range(3))
            for h in range(NH):
                tpa = (tp0 + h) % 3
                tpn = (tpa + 1) % 3
                tl = k.tails[:, l, u, tpa, :]
                btl = k.b_tails[l][u][tpa]
                tln = k.tails[:, l, u, tpn, :]
                btln = k.b_tails[l][u][tpn]
                c0, c1 = h * HC, (h + 1) * HC
                ps, pb = bank(k)
                for kc in range(KC):
                    MM(P, ps[:, 0:HC], w[:, kc * 128:(kc + 1) * 128], k.hT[:, kc, c0:c1], kc == 0, kc == KC - 1,
                       [wb, k.b_hT[kc][h]], [pb])
                ai = (j % 2) * 4 + s * 2 + h
                a = acc[:, ai, :]
                ba = b_acc[ai]
                CP(P, "act", tln, ps[:, HC - 2:HC], [pb], [btln])
                ACT(P, a, ps[:, 0:HC], AF.Identity, [pb, k.b_vecs], [ba], bias=cb, scale=w2)
                STT(P, a[:, 1:HC], ps[:, 0:HC - 1], w1, a[:, 1:HC], ALU.mult, ALU.add, [pb, ba, k.b_vecs], [ba])
                STT(P, a[:, 2:HC], ps[:, 0:HC - 2], w0, a[:, 2:HC], ALU.mult, ALU.add, [pb, ba, k.b_vecs], [ba])
                STT(P, a[:, 0:2], tl[:, 0:2], w0, a[:, 0:2], ALU.mult, ALU.add, [btl, ba, k.b_vecs], [ba])
                STT(P, a[:, 0:1], tl[:, 1:2], w1, a[:, 0:1], ALU.mult, ALU.add, [btl, ba, k.b_vecs], [ba])
        for h in range(NH):
            c0, c1 = h * HC, (h + 1) * HC
            ag = (j % 2) * 4 + h
            av = (j % 2) * 4 + 2 + h
            g = gi % 2
            gi += 1
            ACT(P, gl[:, g, :], acc[:, ag, :], AF.Gelu_apprx_tanh, [b_acc[ag]], [b_gl[g]])
            TT(P, "pool", gT[:, j, c0:c1], gl[:, g, :], acc[:, av, :], ALU.mult, [b_gl[g], b_acc[av]], [b_gT[j][h]])
    k.tpar[l] = (tp0 + NH) % 3
    if T.last:
        tpf = k.tpar[l]
        dst = k.cvp if T.kind == "p" else k.cvs
        for s_ in range(2):
            for t_ in range(2):
                DMA(P, "act", dst[l, T.idx, t_, s_ * DFF:(s_ + 1) * DFF].rearrange("(j p) -> p j", p=128),
                    k.tails[:, l, :, tpf, :].rearrange("p (j s) t -> p j s t", s=2)[:, :, s_, t_],
                    [k.b_tails[l][u][tpf] for u in range(NUP)], [], allow_slow_non_contiguous=True)
    for m in range(KC):
        w, wb = getw(k, 2, ("D", l, m))
        for h in range(NH):
            c0, c1 = h * HC, (h + 1) * HC
            ps, pb = bank(k)
            for kc in range(NPAIR):
                MM(P, ps[:, 0:HC], w[:, kc * 128:(kc + 1) * 128], gT[:, kc, c0:c1], kc == 0, kc == NPAIR - 1,
                   [wb, b_gT[kc][h]], [pb])
            post_evac(k, T, m, h, ps[:, 0:HC], [pb], None)
    post_finish(k, T, l * 4 + 3)


def tiles_of(k):
    TTP = min(1024, k.SEQ)
    tl = []
    for b in range(k.NP):
        n = k.SEQ // TTP
        for i in range(n):
            tl.append(Tile("p", b, i * TTP, TTP, i == 0, i == n - 1, i * TTP))
    for b in range(k.NS):
        tl.append(Tile("s", b, 0, k.DEC, True, True, k.SEQ))
    return tl


def program(k):
    P = k.P
    setup(k)
    if k.nlayers >= 2:
        ssm_setup(k)
    import os
    st = os.environ.get("KSTAGES", "xaf")
    for T in tiles_of(k):
        load_x(k, T)
        if "a" in st:
            attn_stage(k, T)
        if "f" in st:
            ffn_stage(k, T, 0)
        if k.nlayers >= 2:
            ssm_stage(k, T)
            ffn_stage(k, T, 1)
        store_y(k, T)
    P.fence()
    if not k.record:
        P.emit()


def make_nc(NP, SEQ, NS, DEC, nlayers=2):
    k1 = build(NP, SEQ, NS, DEC, nlayers)
    program(k1)
    k2 = build(NP, SEQ, NS, DEC, nlayers, wplan=k1.wplan)
    program(k2)
    return k2.nc


def _chunk_w(W):
    K_ = W.shape[0]
    return np.ascontiguousarray(W.reshape(K_ // 128, 128, W.shape[1]).transpose(1, 0, 2).reshape(128, -1))


def _swap_idx():
    idx = np.arange(128)
    d = idx % 64
    part = np.where(d < 8, idx + 8, np.where(d < 16, idx - 8, idx))
    return part


def host_consts(inp, SEQ, DEC):
    f32 = np.float32
    w_qkv = np.asarray(inp["w_qkv"][0], f32)
    b_qkv = np.asarray(inp["b_qkv"][0], f32)
    sw = _swap_idx()
    wA = np.zeros((24, 128, 1024), f32)
    vecs = np.zeros((128, V_N), f32)
    for i in range(12):
        if i < 8:
            cols = np.arange(128 * i, 128 * i + 128)
        else:
            hk = i - 8
            c = np.arange(1024 + 64 * hk, 1024 + 64 * hk + 64)
            cols = np.concatenate([c, c])
        wA[2 * i] = _chunk_w(w_qkv[:, cols])
        wA[2 * i + 1] = _chunk_w(w_qkv[:, cols[sw]])
        vecs[:, V_BQK + 2 * i] = b_qkv[cols]
        vecs[:, V_BQK + 2 * i + 1] = b_qkv[cols[sw]]
    wv = w_qkv[:, 1280:1536]
    wV = np.ascontiguousarray(wv.reshape(8, 128, 256).transpose(1, 0, 2).reshape(128, 2048))
    w_o = np.asarray(inp["w_o"][0], f32)
    wO = np.stack([_chunk_w(w_o[:, 128 * m:128 * m + 128]) for m in range(8)])
    w_up = np.asarray(inp["w_up"], f32)
    w_dn = np.asarray(inp["w_down"], f32)
    wUP = np.zeros((2, NUP, 128, 1024), f32)
    wDN = np.zeros((2, 8, 128, DFF), f32)
    conv_w = np.asarray(inp["conv_w"], f32)
    conv_b = np.asarray(inp["conv_b"], f32)
    for l in range(2):
        for u in range(NUP):
            cb = (u % 2) * DFF + (u // 2) * 128
            wUP[l, u] = _chunk_w(w_up[l][:, cb:cb + 128])
            vecs[:, V_CB + l * NUP + u] = conv_b[l, cb:cb + 128]
            for t in range(3):
                vecs[:, V_CW + (l * NUP + u) * 3 + t] = conv_w[l, t, cb:cb + 128]
        for m in range(8):
            wDN[l, m] = _chunk_w(w_dn[l][:, 128 * m:128 * m + 128])
    w_glu = np.asarray(inp["w_glu"][0], f32)
    b_glu = np.asarray(inp["b_glu"][0], f32)
    wGL = np.zeros((16, 128, 1024), f32)
    for m in range(8):
        for s in range(2):
            c0 = s * 1024 + 128 * m
            wGL[2 * m + s] = _chunk_w(w_glu[:, c0:c0 + 128])
            vecs[:, V_BGL + 2 * m + s] = b_glu[c0:c0 + 128]
    gains = [inp["g_pre_mix"][0], inp["g_post_mix"][0], inp["g_pre_ffn"][0], inp["g_post_ffn"][0],
             inp["g_pre_mix"][1], inp["g_post_mix"][1], inp["g_pre_ffn"][1], inp["g_post_ffn"][1]]
    for v, g in enumerate(gains):
        vecs[:, V_G + v * 8:V_G + v * 8 + 8] = np.asarray(g, f32).reshape(8, 128).T
    vecs[:, V_BO:V_BO + 8] = np.asarray(inp["b_o"][0], f32).reshape(8, 128).T
    vecs[:, V_SD:V_SD + 8] = np.asarray(inp["ssm_d"][0], f32).reshape(8, 128).T
    vecs[:, V_EPS] = EPS
    bvbc = np.tile(b_qkv[1280:1536][None, :], (128, 1)).astype(f32)
    sinks = np.asarray(inp["attn_sinks"][0], f32)
    sinkarr = np.zeros((128, 8), f32)
    for p in range(128):
        for hk in range(4):
            for mp in range(2):
                sinkarr[p, hk * 2 + mp] = sinks[4 * hk + 2 * mp + p // 64]
    half = 8
    inv_freq = np.power(f32(500000.0), -np.arange(half, dtype=f32) * f32(2.0 / 16)).astype(f32)
    pos = np.concatenate([np.arange(SEQ), PAST + np.arange(DEC)]).astype(f32)
    ang = (pos[:, None] * inv_freq[None, :]).astype(f32)
    cs, sn = np.cos(ang).astype(f32), np.sin(ang).astype(f32)
    rope = np.zeros((2, 128, SEQ + DEC), f32)
    rope[0] = 1.0
    for p in range(128):
        d = p % 64
        if d < 8:
            rope[0, p] = cs[:, d]
            rope[1, p] = -sn[:, d]
        elif d < 16:
            rope[0, p] = cs[:, d - 8]
            rope[1, p] = sn[:, d - 8]
    permsw = np.zeros((128, 128), f32)
    permsw[sw, np.arange(128)] = 1.0
    r_ = np.arange(128) // 16
    mask = (r_[None, :] >= r_[:, None]).astype(f32)
    lam = np.stack([np.asarray(inp["ssm_lam_re"][0], f32).T, np.asarray(inp["ssm_lam_im"][0], f32).T,
                    np.tile(np.asarray(inp["ssm_log_dt"][0], f32)[None, :], (64, 1))]).astype(f32)
    bc = np.stack([np.asarray(inp["ssm_b_re"][0], f32).transpose(1, 0, 2).reshape(64, 1024),
                   np.asarray(inp["ssm_b_im"][0], f32).transpose(1, 0, 2).reshape(64, 1024),
                   np.asarray(inp["ssm_c_re"][0], f32).transpose(2, 0, 1).reshape(64, 1024),
                   np.asarray(inp["ssm_c_im"][0], f32).transpose(2, 0, 1).reshape(64, 1024)]).astype(f32)
    return dict(wA=wA, wV=wV, wO=wO, wUP=wUP, wDN=wDN, wGL=wGL, vecs=vecs, bvbc=bvbc, sinkarr=sinkarr,
                ident=np.eye(128, dtype=f32), rope=rope, ssmmask=mask, permsw=permsw, ssmlam=np.ascontiguousarray(lam),
                ssmbc=np.ascontiguousarray(bc))


def run(inp, n_cores, nlayers=2):
    f32 = np.float32
    xp = np.asarray(inp["x_prompt"], f32)
    xs = np.asarray(inp["x_sample"], f32)
    B, SEQ, _ = xp.shape
    BS, DEC, _ = xs.shape
    NP, NS = B // n_cores, BS // n_cores
    consts = host_consts(inp, SEQ, DEC)
    ck = np.asarray(inp["cache_k"][0], f32).reshape(BS, 128, 256)
    cv = np.asarray(inp["cache_v"][0], f32).reshape(BS, 128, 256)
    sre = np.asarray(inp["state_ssm_re"][0], f32)
    sim = np.asarray(inp["state_ssm_im"][0], f32)
    cconv = np.asarray(inp["cache_conv"], f32)
    in_maps = []
    for c in range(n_cores):
        m = dict(consts)
        m["xp"] = np.ascontiguousarray(xp[c * NP:(c + 1) * NP])
        m["xs"] = np.ascontiguousarray(xs[c * NS:(c + 1) * NS])
        m["ck"] = np.ascontiguousarray(ck[c * NS:(c + 1) * NS])
        m["cv"] = np.ascontiguousarray(cv[c * NS:(c + 1) * NS])
        m["sre"] = np.ascontiguousarray(sre[c * NS:(c + 1) * NS])
        m["sim"] = np.ascontiguousarray(sim[c * NS:(c + 1) * NS])
        m["cconv"] = np.ascontiguousarray(cconv[:, c * NS:(c + 1) * NS])
        in_maps.append(m)
    nc = make_nc(NP, SEQ, NS, DEC, nlayers)
    res = run_bass_kernel_spmd(nc, in_maps, core_ids=list(range(n_cores)))
    R = res.results

    def cat(name, axis=0):
        return np.concatenate([np.asarray(r[name]) for r in R], axis=axis)
    y_p = cat("yp")
    y_s = cat("ys")
    k_p = cat("kp").reshape(1, B, 128, 4, 64)
    v_p = cat("vp").reshape(1, B, 128, 4, 64)
    k_s = cat("ks").reshape(1, BS, 128, 4, 64)
    v_s = cat("vs").reshape(1, BS, 128, 4, 64)
    re_p = cat("rep")[None]
    im_p = cat("imp")[None]
    re_s = cat("res")[None]
    im_s = cat("ims")[None]
    cv_p = cat("cvp", 1)
    cv_s = cat("cvs", 1)
    return tuple(np.ascontiguousarray(a, dtype=f32) for a in
                 (y_p, y_s, k_p, v_p, k_s, v_s, re_p, im_p, re_s, im_s, cv_p, cv_s))


def kernel(**inputs):
    return run(inputs, N_CORES, 2)


def ssm_setup(k):
    P = k.P
    arena_switch(k)
    TWO_PI = float(2.0 * np.pi)
    o = 0

    def cv(shape, dt=F32):
        nonlocal o
        ap, o = carve(k, o, shape, dt)
        return ap
    LR, LI, DT = cv([64]), cv([64]), cv([64])
    LRDT, LIDT = cv([64]), cv([64])
    ANG = cv([64, 8])
    MPt, MNt, MPn = cv([64, 8]), cv([64, 8]), cv([64, 8])
    CS, SN = cv([64, 8]), cv([64, 8])
    Fq, Nf = cv([64, 8]), cv([64, 8])
    Ni = cv([64, 8], I32)
    CR, CI, A8R, A8I = cv([64]), cv([64]), cv([64]), cv([64])
    w1, w2, w3 = cv([64]), cv([64]), cv([64])
    Bq = [cv([8, 16]) for _ in range(4)]
    BBR, BBI = cv([8, 16]), cv([8, 16])
    FR, FI = cv([8, 8, 16]), cv([8, 8, 16])
    GR, GI = cv([8, 8, 16]), cv([8, 8, 16])
    ER, EI = cv([8, 8, 16]), cv([8, 8, 16])
    T1, T2 = cv([8, 8, 16]), cv([8, 8, 16])
    stw = cv([8, 256], BF16)
    gst = cv([8, 2, 128], BF16)
    mask = cv([128])
    b = nb(k)
    bm = nb(k)
    bst = nb(k)
    bg = nb(k)
    Q = slice(0, 64)
    DMA(P, "act", LR[Q], k.lam_d[0], [], [b])
    DMA(P, "act", LI[Q], k.lam_d[1], [], [b])
    DMA(P, "act", DT[Q], k.lam_d[2], [], [b])
    DMA(P, "act", mask, k.mask_d[:, :], [], [bm])
    ACT(P, DT[Q], DT[Q], AF.Exp, [b], [b])
    TT(P, "dve", LRDT[Q], LR[Q], DT[Q], ALU.mult, [b], [b])
    TT(P, "dve", LIDT[Q], LI[Q], DT[Q], ALU.mult, [b], [b])
    for kk in range(1, 9):
        TS(P, "dve", ANG[Q, :, kk - 1], LIDT[Q], float(kk), ALU.mult, [b], [b])
        ACT(P, MPt[Q, :, kk - 1], LRDT[Q], AF.Exp, [b], [b], scale=float(kk))
        ACT(P, MNt[Q, :, kk - 1], LRDT[Q], AF.Exp, [b], [b], scale=float(-kk))
    TS(P, "dve", MPn[Q], MPt[Q], -1.0, ALU.mult, [b], [b])
    for (dst, shift) in ((SN, 0.0), (CS, 0.25)):
        TS(P, "dve", Fq[Q], ANG[Q], 1.0 / TWO_PI, ALU.mult, [b], [b], s2=shift, op1=ALU.add)
        CP(P, "dve", Ni[Q], Fq[Q], [b], [b])
        CP(P, "dve", Nf[Q], Ni[Q], [b], [b])
        TT(P, "dve", Fq[Q], Fq[Q], Nf[Q], ALU.subtract, [b], [b])
        TS(P, "dve", Nf[Q], Fq[Q], 0.5, ALU.is_gt, [b], [b])
        TT(P, "dve", Fq[Q], Fq[Q], Nf[Q], ALU.subtract, [b], [b])
        TS(P, "dve", Nf[Q], Fq[Q], -0.5, ALU.is_lt, [b], [b])
        TT(P, "dve", Fq[Q], Fq[Q], Nf[Q], ALU.add, [b], [b])
        TS(P, "dve", Fq[Q], Fq[Q], -0.49999997, ALU.max, [b], [b], s2=0.49999997, op1=ALU.min)
        ACT(P, dst[Q], Fq[Q], AF.Sin, [b], [b], scale=TWO_PI)
    TT(P, "dve", w1[Q], MPt[Q, :, 0], CS[Q, :, 0], ALU.mult, [b], [b])
    TS(P, "dve", w1[Q], w1[Q], -1.0, ALU.add, [b], [b])
    TT(P, "dve", w2[Q], MPt[Q, :, 0], SN[Q, :, 0], ALU.mult, [b], [b])
    TT(P, "dve", w3[Q], LR[Q], LR[Q], ALU.mult, [b], [b])
    TT(P, "dve", CR[Q], LI[Q], LI[Q], ALU.mult, [b], [b])
    TT(P, "dve", w3[Q], w3[Q], CR[Q], ALU.add, [b], [b])
    RECIP(P, w3[Q], w3[Q], [b], [b])
    TT(P, "dve", CR[Q], w1[Q], LR[Q], ALU.mult, [b], [b])
    TT(P, "dve", CI[Q], w2[Q], LI[Q], ALU.mult, [b], [b])
    TT(P, "dve", CR[Q], CR[Q], CI[Q], ALU.add, [b], [b])
    TT(P, "dve", CR[Q], CR[Q], w3[Q], ALU.mult, [b], [b])
    TT(P, "dve", CI[Q], w2[Q], LR[Q], ALU.mult, [b], [b])
    TT(P, "dve", w1[Q], w1[Q], LI[Q], ALU.mult, [b], [b])
    TT(P, "dve", CI[Q], CI[Q], w1[Q], ALU.subtract, [b], [b])
    TT(P, "dve", CI[Q], CI[Q], w3[Q], ALU.mult, [b], [b])
    TT(P, "dve", A8R[Q], MPt[Q, :, 7], CS[Q, :, 7], ALU.mult, [b], [b])
    TT(P, "dve", A8I[Q], MPt[Q, :, 7], SN[Q, :, 7], ALU.mult, [b], [b])
    DMA(P, "act", k.a8d[0], A8R[Q], [b], [k.b_a8d])
    DMA(P, "act", k.a8d[1], A8I[Q], [b], [k.b_a8d])
    ba = Buf()
    for gl in range(2):
        for (comp, sl) in ((0, (0, 0)), (0, (0, 1)), (1, (1, 1)), (1, (1, 0))):
            src = k.a8d[comp].rearrange("p (gh gl) -> gl p gh", gl=2)[gl]
            DMA(P, "act", k.a12[64 * gl:64 * gl + 64, sl[0], sl[1], :], src, [k.b_a8d], [ba], allow_slow_non_contiguous=True)
    TS(P, "dve", k.a12[:, 1, 0, :], k.a12[:, 1, 0, :], -1.0, ALU.mult, [ba], [k.b_const])
    for n_, srcap in ((2, CS[Q, :, 7]), (3, SN[Q, :, 7]), (4, MPt[Q, :, 7])):
        CP(P, "dve", w1[Q], srcap, [b], [b])
        DMA(P, "act", k.a8d[n_], w1[Q], [b], [k.b_a8d])
    for n_ in range(3):
        for gl in range(2):
            src = k.a8d[2 + n_].rearrange("p (gh gl) -> gl p gh", gl=2)[gl]
            DMA(P, "act", k.rot1[64 * gl:64 * gl + 64, n_, :], src, [k.b_a8d], [ba], allow_slow_non_contiguous=True)
    RC = cv([32, 65])
    RS = cv([32, 65])
    U1 = T1.rearrange("p a b c -> p (a b c)").rearrange("p (a b) -> p a b", b=32)
    U2 = T2.rearrange("p a b c -> p (a b c)").rearrange("p (a b) -> p a b", b=32)
    MEMSET(P, "dve", RC[:, :, 0], 1.0, [b])
    MEMSET(P, "dve", RS[:, :, 0], 0.0, [b])
    CP(P, "dve", RC[:, :, 1], k.rot1[:, 0, :], [ba, b], [b])
    CP(P, "dve", RS[:, :, 1], k.rot1[:, 1, :], [ba, b], [b])
    span = 1
    while span < 64:
        n = span
        cS = RC[:, :, span:span + 1].broadcast_to([128, 32, n])
        sS = RS[:, :, span:span + 1].broadcast_to([128, 32, n])
        a_c, a_s = RC[:, :, 1:1 + n], RS[:, :, 1:1 + n]
        u1, u2 = U1[:, :, 0:n], U2[:, :, 0:n]
        TT(P, "dve", u1, a_c, cS, ALU.mult, [b], [b])
        TT(P, "dve", u2, a_s, sS, ALU.mult, [b], [b])
        TT(P, "dve", RC[:, :, span + 1:span + 1 + n], u1, u2, ALU.subtract, [b], [b])
        TT(P, "dve", u1, a_c, sS, ALU.mult, [b], [b])
        TT(P, "dve", u2, a_s, cS, ALU.mult, [b], [b])
        TT(P, "dve", RS[:, :, span + 1:span + 1 + n], u1, u2, ALU.add, [b], [b])
        span *= 2
    DMA(P, "act", k.rtab[0], RC.rearrange("p a b -> p (a b)"), [b], [k.b_rtab])
    DMA(P, "act", k.rtab[1], RS.rearrange("p a b -> p (a b)"), [b], [k.b_rtab])

    def bj(ap2, g0):
        return ap2[Q, g0:g0 + 8].unsqueeze(2).unsqueeze(3).broadcast_to([64, 8, 8, 16])

    def bk(tab, g0):
        return tab[Q, g0:g0 + 8, :].unsqueeze(3).broadcast_to([64, 8, 8, 16])

    def br(ap3):
        return ap3[Q].unsqueeze(2).broadcast_to([64, 8, 8, 16])
    btab = nb(k).inherit([b])
    bcin = nb(k)
    bGc = nb(k).inherit([b])
    T3 = RC.rearrange("p a b -> p (a b)")[:, 0:1024].rearrange("p (a b c) -> p a b c", b=8, c=16)
    T4 = RS.rearrange("p a b -> p (a b)")[:, 0:1024].rearrange("p (a b c) -> p a b c", b=8, c=16)
    for gb in range(8):
        g0 = gb * 8
        for n in range(4):
            DMA(P, "act", Bq[n][Q], k.bc_d[n][:, g0 * 16:(g0 + 8) * 16].rearrange("p (g j) -> p g j", j=16), [],
                [b] if n < 2 else [bcin])
        c8 = CR[Q, g0:g0 + 8].unsqueeze(2).broadcast_to([64, 8, 16])
        i8 = CI[Q, g0:g0 + 8].unsqueeze(2).broadcast_to([64, 8, 16])
        t1s, t2s = T1[Q, :, 0, :], T2[Q, :, 0, :]
        TT(P, "dve", t1s, Bq[0][Q], c8, ALU.mult, [b], [b])
        TT(P, "dve", t2s, Bq[1][Q], i8, ALU.mult, [b], [b])
        TT(P, "dve", BBR[Q], t1s, t2s, ALU.subtract, [b], [b])
        TT(P, "dve", t1s, Bq[1][Q], c8, ALU.mult, [b], [b])
        TT(P, "dve", t2s, Bq[0][Q], i8, ALU.mult, [b], [b])
        TT(P, "dve", BBI[Q], t1s, t2s, ALU.add, [b], [b])
        TT(P, "dve", T1[Q], bk(CS, g0), br(BBR), ALU.mult, [b], [b])
        TT(P, "dve", T2[Q], bk(SN, g0), br(BBI), ALU.mult, [b], [b])
        TT(P, "dve", T1[Q], T1[Q], T2[Q], ALU.add, [b], [b])
        TT(P, "dve", FR[Q], T1[Q], bk(MNt, g0), ALU.mult, [b], [b])
        TT(P, "dve", T1[Q], bk(CS, g0), br(BBI), ALU.mult, [b], [b])
        TT(P, "dve", T2[Q], bk(SN, g0), br(BBR), ALU.mult, [b], [b])
        TT(P, "dve", T1[Q], T1[Q], T2[Q], ALU.subtract, [b], [b])
        TT(P, "dve", FI[Q], T1[Q], bk(MNt, g0), ALU.mult, [b], [b])
        rdG = [btab, bcin, bGc]
        TT(P, "pool", T3[Q], bk(CS, g0), br(Bq[2]), ALU.mult, rdG, [bGc])
        TT(P, "pool", T4[Q], bk(SN, g0), br(Bq[3]), ALU.mult, rdG, [bGc])
        TT(P, "pool", T3[Q], T3[Q], T4[Q], ALU.subtract, rdG, [bGc])
        TT(P, "pool", GR[Q], T3[Q], bk(MPt, g0), ALU.mult, rdG, [bGc])
        TT(P, "pool", T3[Q], bk(SN, g0), br(Bq[2]), ALU.mult, rdG, [bGc])
        TT(P, "pool", T4[Q], bk(CS, g0), br(Bq[3]), ALU.mult, rdG, [bGc])
        TT(P, "pool", T3[Q], T3[Q], T4[Q], ALU.add, rdG, [bGc])
        TT(P, "pool", GI[Q], T3[Q], bk(MPn, g0), ALU.mult, rdG, [bGc])
        TT(P, "dve", T1[Q], FR[Q], bj(A8R, g0), ALU.mult, [b], [b])
        TT(P, "dve", T2[Q], FI[Q], bj(A8I, g0), ALU.mult, [b], [b])
        TT(P, "dve", ER[Q], T1[Q], T2[Q], ALU.subtract, [b], [b])
        TT(P, "dve", T1[Q], FI[Q], bj(A8R, g0), ALU.mult, [b], [b])
        TT(P, "dve", T2[Q], FR[Q], bj(A8I, g0), ALU.mult, [b], [b])
        TT(P, "dve", EI[Q], T1[Q], T2[Q], ALU.add, [b], [b])
        for gl in range(8):
            ps, pb = bank(k)
            fr = FR[Q, gl, :, :].rearrange("p a b -> p (a b)")
            fi = FI[Q, gl, :, :].rearrange("p a b -> p (a b)")
            gr = GR[Q, gl, :, :].rearrange("p a b -> p (a b)")
            gi = GI[Q, gl, :, :].rearrange("p a b -> p (a b)")
            MM(P, ps[:, 0:128], fr, gr, True, False, [b, bGc], [pb])
            MM(P, ps[:, 0:128], fi, gi, False, True, [b, bGc], [pb])
            TT(P, "dve", stw[:, gl, 0:128], ps[:, 0:128], mask, ALU.mult, [pb, bm], [bst])
            ps2, pb2 = bank(k)
            TR(P, ps2[:, 0:64], ER[Q, gl, :, :].rearrange("p a b -> p (a b)"), k.identf[0:64, 0:64], [b, k.b_const], [pb2])
            TR(P, ps2[:, 64:128], EI[Q, gl, :, :].rearrange("p a b -> p (a b)"), k.identf[0:64, 0:64], [b, k.b_const], [pb2])
            CP(P, "act", stw[:, gl, 128:256], ps2[:, 0:128], [pb2], [bst])
        DMA(P, "act", k.ssmW[g0:g0 + 8].rearrange("g p c -> p g c"), stw, [bst], k.b_ssmW[g0:g0 + 8])
        CP(P, "act", gst[Q, :, 0, :], GR[Q].rearrange("p g a b -> p g (a b)"), [bGc], [bg])
        CP(P, "act", gst[Q, :, 1, :], GI[Q].rearrange("p g a b -> p g (a b)"), [bGc], [bg])
        DMA(P, "act", k.ssmG[g0 // 2:g0 // 2 + 4].rearrange("gp (gl p) c -> p (gp gl) c", gl=2),
            gst[Q].rearrange("p g a b -> p g (a b)"), [bg], k.b_ssmG[g0 // 2:g0 // 2 + 4])


def ssm_stage(k, T):
    P = k.P
    TT_, NH, HC = T.TT, T.NH, T.HC
    SUBT = min(TT_, 512)
    NSUB = TT_ // SUBT
    NC = SUBT // 8
    prenorm(k, T, 4)
    arena_switch(k)
    o = 0
    tm, o = carve(k, o, [8192], BF16)
    Usup, o = carve(k, o, [NG, NC], BF16)
    Gs, o = carve(k, o, [2, 32, NC + 1], F32)
    RCt, o = carve(k, o, [32, 65], F32)
    RSt, o = carve(k, o, [32, 65], F32)
    T1, o = carve(k, o, [16, NC], F32)
    T2, o = carve(k, o, [16, NC], F32)
    tB, o = carve(k, o, [2, 4, NC], F32)
    sct, o = carve(k, o, [2, 2, 32], F32)
    Hbf, o = carve(k, o, [2, 32, NC], BF16)
    Ysb, o = carve(k, o, [2, 2, 4, NC], BF16)
    tz, o = carve(k, o, [2, 512], F32)
    tm_g = tm.rearrange("p (g r j) -> p g r j", r=8, j=16)
    tm_r = tm.rearrange("p (r g j) -> p r g j", g=NG, j=16)
    b_tz = nb(k, 2)
    tzi = 0
    b_rt = nb(k)
    DMA(P, "pool", RCt.rearrange("p a b -> p (a b)"), k.rtab[0], [k.b_rtab], [b_rt])
    DMA(P, "pool", RSt.rearrange("p a b -> p (a b)"), k.rtab[1], [k.b_rtab], [b_rt])
    b_T = nb(k, 2)
    b_tB = nb(k, 2)
    b_G = nb(k, 8, 2)
    b_R = nb(k, 2, 32)
    allG = [b_G[g_][c_] for g_ in range(8) for c_ in range(2)] + [b_R[c_][g_] for c_ in range(2) for g_ in range(32)]
    if T.first:
        if T.kind == "p":
            MEMSET(P, "dve", k.hstate[:, :, :], 0.0, [k.b_hstate])
        else:
            for comp, src in ((0, k.sre), (1, k.sim), (2, k.sre)):
                for gl in range(2):
                    DMA(P, "pool", k.hstate[64 * gl:64 * gl + 64, comp, :],
                        src[T.idx].rearrange("(gh gl) p -> gl p gh", gl=2)[gl], [], [k.b_hstate],
                        allow_slow_non_contiguous=True)
    b_tmU = nb(k)
    b_U = nb(k, 8)
    b_S = nb(k)
    b_H = nb(k)
    b_tmY = nb(k, 8)
    b_Y = nb(k, 2, 2)
    for sbi in range(NSUB):
        cb = sbi * SUBT
        hh = cb // 512
        b_tmU.inherit(b_tmY)
        for r in range(8):
            ps, pb = bank(k)
            psb = ps[:, :].bitcast(BF16)
            for kc in range(KC):
                TR(P, psb[0:NC, kc * 128:(kc + 1) * 128], k.hT[:, kc, cb + r:cb + SUBT:8], k.identb[:, :],
                   [k.b_hT[kc][hh], k.b_const], [pb])
            CP(P, "act", tm_g[0:NC, :, r, :], psb[0:NC, 0:1024].rearrange("p (g j) -> p g j", j=16),
               [pb], [b_tmU])
        for gb in range(8):
            ps, pb = bank(k)
            psb = ps[:, :].bitcast(BF16)
            for gl in range(8):
                g = gb * 8 + gl
                TR(P, psb[:, gl * NC:(gl + 1) * NC], tm_g[0:NC, g, :, :].rearrange("p r j -> p (r j)"), k.identb[0:NC, 0:NC],
                   [b_tmU, k.b_const], [pb])
            CP(P, "act", Usup[:, gb * 8:(gb + 1) * 8, :],
               psb[:, 0:8 * NC].rearrange("p (g c) -> p g c", c=NC), [pb], [b_U[gb]])
        for gb in range(8):
            wsw, wswb = getw(k, 2, ("SW", gb * 8))
            wv = wsw.rearrange("p (g c) -> p g c", c=256)
            psr, pbr = bank(k)
            psi, pbi = bank(k)
            for gl in range(8):
                g = gb * 8 + gl
                g2, gpl = g % 2, gl // 2
                MM(P, psr[64 * g2:64 * g2 + 64, gpl * NC:(gpl + 1) * NC], wv[:, gl, 128:192], Usup[:, g, :], True, True,
                   [wswb, b_U[gb]], [pbr])
                MM(P, psi[64 * g2:64 * g2 + 64, gpl * NC:(gpl + 1) * NC], wv[:, gl, 192:256], Usup[:, g, :], True, True,
                   [wswb, b_U[gb]], [pbi])
            gp0 = gb * 4
            rc = RCt[:, gp0:gp0 + 4, 1:NC + 1]
            rs = RSt[:, gp0:gp0 + 4, 1:NC + 1]
            prv = psr[:, 0:4 * NC].rearrange("p (g c) -> p g c", c=NC)
            piv = psi[:, 0:4 * NC].rearrange("p (g c) -> p g c", c=NC)
            gre = Gs[:, 0, gp0:gp0 + 4, 1:NC + 1]
            gim = Gs[:, 1, gp0:gp0 + 4, 1:NC + 1]
            TT(P, "dve", gre, prv, rc, ALU.mult, [pbr, b_rt], [b_G[gb][0]])
            TT(P, "dve", tB[:, 0, :, :], piv, rs, ALU.mult, [pbi, b_rt], [b_tB[0]])
            TT(P, "pool", gre, gre, tB[:, 0, :, :], ALU.add, [b_tB[0]], [b_G[gb][0]])
            TT(P, "dve", gim, piv, rc, ALU.mult, [pbi, b_rt], [b_G[gb][1]])
            TT(P, "dve", tB[:, 1, :, :], prv, rs, ALU.mult, [pbr, b_rt], [b_tB[1]])
            TT(P, "pool", gim, gim, tB[:, 1, :, :], ALU.subtract, [b_tB[1]], [b_G[gb][1]])
        CP(P, "dve", Gs[:, :, :, 0], k.hstate[:, 0:2, :], [k.b_hstate], [b_S])
        for comp in range(2):
            for gp in range(32):
                row = Gs[:, comp, gp, 1:NC + 1]
                m8 = k.rot1[:, 2, gp:gp + 1].broadcast_to([128, NC])
                ini = Gs[:, comp, gp, 0:1]
                P.op("dve", (lambda e, row=row, m8=m8, ini=ini: e.tensor_tensor_scan(out=row, data0=m8, data1=row, initial=ini,
                                                                                      op0=ALU.mult, op1=ALU.add)),
                     [b_S, k.b_const, b_G[gp // 4][comp]], [b_R[comp][gp]])
        for hf in range(2):
            gs_ = slice(16 * hf, 16 * hf + 16)
            rc = RCt[:, gs_, 0:NC]
            rs = RSt[:, gs_, 0:NC]
            gre = Gs[:, 0, gs_, 0:NC]
            gim = Gs[:, 1, gs_, 0:NC]
            TT(P, "dve", T1, rc, gre, ALU.mult, [b_rt] + allG, [b_T[0]])
            TT(P, "pool", T2, rs, gim, ALU.mult, [b_rt] + allG, [b_T[1]])
            TT(P, "dve", Hbf[:, 0, gs_, :], T1, T2, ALU.subtract, [b_T[0], b_T[1]], [b_H])
            TT(P, "dve", T1, rc, gim, ALU.mult, [b_rt] + allG, [b_T[0]])
            TT(P, "pool", T2, rs, gre, ALU.mult, [b_rt] + allG, [b_T[1]])
            TT(P, "dve", Hbf[:, 1, gs_, :], T1, T2, ALU.add, [b_T[0], b_T[1]], [b_H])
        rcl, rsl = RCt[:, :, NC], RSt[:, :, NC]
        grl, gil = Gs[:, 0, :, NC], Gs[:, 1, :, NC]
        u1, u2 = sct[:, 0, 0, :], sct[:, 0, 1, :]
        TT(P, "dve", u1, rcl, grl, ALU.mult, [b_rt] + allG, [b_S])
        TT(P, "dve", u2, rsl, gil, ALU.mult, [b_rt] + allG, [b_S])
        TT(P, "dve", k.hstate[:, 0, :], u1, u2, ALU.subtract, [b_S], [k.b_hstate])
        TT(P, "dve", u1, rcl, gil, ALU.mult, [b_rt] + allG, [b_S])
        TT(P, "dve", u2, rsl, grl, ALU.mult, [b_rt] + allG, [b_S])
        TT(P, "dve", k.hstate[:, 1, :], u1, u2, ALU.add, [b_S], [k.b_hstate])
        for bb in b_tmY:
            bb.inherit([b_tmU])
        yi = 0
        for gb in range(8):
            wsw, wswb = getw(k, 2, ("SW", gb * 8))
            wv = wsw.rearrange("p (g c) -> p g c", c=256)
            wsg, wsgb = getw(k, 1, ("SG", gb * 4))
            gv = wsg.rearrange("p (g c) -> p g c", c=256)
            yb_ = yi % 2
            yi += 1
            for par in range(2):
                ps, pb = bank(k)
                for q in range(4):
                    gl = 2 * q + par
                    g = gb * 8 + gl
                    gp = g // 2
                    out = ps[:, q * NC:(q + 1) * NC]
                    MM(P, out, wv[:, gl, 0:128], Usup[:, g, :], True, False, [wswb, b_U[gb]], [pb])
                    MM(P, out, gv[64 * par:64 * par + 64, q, 0:128], Hbf[64 * par:64 * par + 64, 0, gp, :], False, False,
                       [wsgb, b_H], [pb])
                    MM(P, out, gv[64 * par:64 * par + 64, q, 128:256], Hbf[64 * par:64 * par + 64, 1, gp, :], False, True,
                       [wsgb, b_H], [pb])
                CP(P, "act", Ysb[:, yb_, par, :, :], ps[:, 0:4 * NC].rearrange("p (g c) -> p g c", c=NC),
                   [pb], [b_Y[yb_][par]])
            ps, pb = bank(k)
            psb = ps[:, :].bitcast(BF16)
            for gl in range(8):
                TR(P, psb[0:NC, gl * 128:(gl + 1) * 128], Ysb[:, yb_, gl % 2, gl // 2, :], k.identb[:, :],
                   [b_Y[yb_][gl % 2], k.b_const], [pb])
            CP(P, "act", tm_r[0:NC, :, gb * 8:(gb + 1) * 8, :],
               psb[0:NC, 0:1024].rearrange("p (g r i) -> p r g i", r=8, i=16), [pb], [b_tmY[gb]])
        for kc in range(KC):
            ps, pb = bank(k)
            psb = ps[:, :].bitcast(BF16)
            for r in range(8):
                TR(P, psb[:, r * NC:(r + 1) * NC], tm_r[0:NC, r, kc * 8:(kc + 1) * 8, :].rearrange("p g i -> p (g i)"),
                   k.identb[0:NC, 0:NC], [b_tmY[kc], k.b_const], [pb])
            ti = tzi % 2
            tzi += 1
            hv = k.hT[:, kc, cb:cb + SUBT].rearrange("p (c r) -> p r c", r=8)
            tv = tz[:, ti, 0:SUBT].rearrange("p (r c) -> p r c", c=NC)
            STT(P, tv, hv, k.vecs[:, V_SD + kc:V_SD + kc + 1], psb[:, 0:SUBT].rearrange("p (r c) -> p r c", c=NC),
                ALU.mult, ALU.add, [k.b_hT[kc][hh], pb, k.b_vecs], [b_tz[ti]])
            ACT(P, hv, tv, AF.Gelu_apprx_tanh, [b_tz[ti]], [k.b_hT[kc][hh]])
    if T.last:
        dre, dim_ = (k.rep, k.imp) if T.kind == "p" else (k.res, k.ims)
        for comp, dst in ((0, dre), (1, dim_)):
            for gl in range(2):
                DMA(P, "act", dst[T.idx].rearrange("(gh gl) p -> gl p gh", gl=2)[gl], k.hstate[64 * gl:64 * gl + 64, comp, :],
                    [k.b_hstate], [], allow_slow_non_contiguous=True)
    b_g = b_tz
    gi = 0
    for m in range(KC):
        wa, wab = getw(k, 1, ("G", 2 * m))
        wg, wgb = getw(k, 1, ("G", 2 * m + 1))
        for h in range(NH):
            c0, c1 = h * HC, (h + 1) * HC
            psa, pba = bank(k)
            psg, pbg = bank(k)
            for kc in range(KC):
                MM(P, psa[:, 0:HC], wa[:, kc * 128:(kc + 1) * 128], k.hT[:, kc, c0:c1], kc == 0, kc == KC - 1,
                   [wab, k.b_hT[kc][h]], [pba])
            for kc in range(KC):
                MM(P, psg[:, 0:HC], wg[:, kc * 128:(kc + 1) * 128], k.hT[:, kc, c0:c1], kc == 0, kc == KC - 1,
                   [wgb, k.b_hT[kc][h]], [pbg])
            ti = gi % 2
            gi += 1
            ACT(P, tz[:, ti, 0:HC], psg[:, 0:HC], AF.Sigmoid, [pbg, k.b_vecs], [b_g[ti]],
                bias=k.vecs[:, V_BGL + 2 * m + 1:V_BGL + 2 * m + 2])
            STT(P, tz[:, ti, 0:HC], psa[:, 0:HC], k.vecs[:, V_BGL + 2 * m:V_BGL + 2 * m + 1], tz[:, ti, 0:HC], ALU.add, ALU.mult,
                [pba, b_g[ti], k.b_vecs], [b_g[ti]])
            post_evac(k, T, m, h, tz[:, ti, 0:HC], [b_g[ti]], None)
    post_finish(k, T, 5)
```
